# Optimizing a Trainium2 kernel written in Bass

```python
import jax, jax.numpy as jnp
from jax import lax
import numpy as np

D_MODEL = 2048
BATCH = 4
SEQ = 2048
DEPTH = 2
DEC_BATCH = 128
DEC_SEQ = 1
PAST_LEN = 16384
PAGE_SIZE = 128

MIX_WIDTH = D_MODEL
W_A = MIX_WIDTH // 2
H_A = 8
HD_A = W_A // H_A
K_A = 4
LRU_C = 8.0
W_B = MIX_WIDTH // 4
K_B = 31
W_C = MIX_WIDTH // 4
H_C = 4
HD_C = W_C // H_C
CHUNK = 128
D_IN = 2 * (W_A + W_B + W_C)
D_FF = 4 * D_MODEL
EPS = 1e-6

kernel_name = "hybrid_rglru_convmod_chunkmlp_decode"


def rms_norm(x, g):
    x32 = x.astype(jnp.float32)
    y = x32 * lax.rsqrt(jnp.mean(x32 * x32, axis=-1, keepdims=True) + EPS)
    return (y * g.astype(jnp.float32)).astype(x.dtype)


def layer_norm(x, g, b):
    x32 = x.astype(jnp.float32)
    xc = x32 - jnp.mean(x32, axis=-1, keepdims=True)
    y = xc * lax.rsqrt(jnp.mean(xc * xc, axis=-1, keepdims=True) + EPS)
    return (y * g.astype(jnp.float32) + b.astype(jnp.float32)).astype(x.dtype)


def causal_dwconv(x_full, w):
    c = x_full.shape[-1]
    return lax.conv_general_dilated(
        x_full, w.astype(x_full.dtype)[:, None, :], window_strides=(1,), padding='VALID',
        dimension_numbers=('NWC', 'WIO', 'NWC'), feature_group_count=c)


def rg_lru(x, h0, w_r, b_r, w_i, b_i, lam):
    bsz, seq_len, _ = x.shape
    xh = x.reshape(bsz, seq_len, H_A, HD_A)
    r = jax.nn.sigmoid(jnp.einsum('blhi,hij->blhj', xh, w_r.astype(x.dtype)).reshape(bsz, seq_len, W_A).astype(jnp.float32) + b_r.astype(jnp.float32))
    i = jax.nn.sigmoid(jnp.einsum('blhi,hij->blhj', xh, w_i.astype(x.dtype)).reshape(bsz, seq_len, W_A).astype(jnp.float32) + b_i.astype(jnp.float32))
    log_a = -LRU_C * r * jax.nn.softplus(-lam.astype(jnp.float32))
    a = jnp.exp(log_a)
    u = jnp.sqrt(-jnp.expm1(2.0 * log_a)) * (i * x.astype(jnp.float32))

    def step(h, au):
        a_t, u_t = au
        h = a_t * h + u_t
        return h, h

    h_last, hs = lax.scan(step, h0.astype(jnp.float32), (jnp.swapaxes(a, 0, 1), jnp.swapaxes(u, 0, 1)))
    return jnp.swapaxes(hs, 0, 1).astype(x.dtype), h_last


def spatial_gate(u, v, w_s, b_s):
    bsz, seq_len, _ = v.shape
    pad = (-seq_len) % CHUNK
    vp = jnp.pad(v, ((0, 0), (0, pad), (0, 0)))
    n_chunks = (seq_len + pad) // CHUNK
    vr = vp.reshape(bsz, n_chunks, CHUNK, H_C, HD_C)
    mask = jnp.tril(jnp.ones((CHUNK, CHUNK), dtype=w_s.dtype))
    w = (w_s * mask[None]).astype(v.dtype)
    mixed = jnp.einsum('hts,bcshd->bcthd', w, vr) + b_s.T.astype(v.dtype)[None, None, :, :, None]
    mixed = mixed.reshape(bsz, n_chunks * CHUNK, W_C)[:, :seq_len]
    return u * mixed


def hybrid_layer(x, conv_a_buf, h0, conv_b_buf, p, l):
    hn = rms_norm(x, p['norm_mix'][l])
    z = jnp.einsum('bld,de->ble', hn, p['w_in'][l].astype(x.dtype))
    xa, ga, xb, gb, zc = jnp.split(z, [W_A, 2 * W_A, 2 * W_A + W_B, 2 * W_A + 2 * W_B], axis=-1)
    xa_full = jnp.concatenate([conv_a_buf.astype(x.dtype), xa], axis=1)
    xa_conv = causal_dwconv(xa_full, p['conv_a_w'][l]) + p['conv_a_b'][l].astype(x.dtype)
    y_a, h_last = rg_lru(xa_conv, h0, p['gate_r_w'][l], p['gate_r_b'][l], p['gate_i_w'][l], p['gate_i_b'][l], p['lru_lambda'][l])
    y_a = y_a * jax.nn.gelu(ga)
    ub = xb * jax.nn.sigmoid(gb)
    ub_full = jnp.concatenate([conv_b_buf.astype(x.dtype), ub], axis=1)
    y_b = causal_dwconv(ub_full, p['conv_b_w'][l])
    y_b = jax.nn.silu(layer_norm(y_b, p['ln_b_g'][l], p['ln_b_b'][l]))
    uc, vc = jnp.split(jax.nn.gelu(zc), 2, axis=-1)
    vn = layer_norm(vc, p['sgu_ln_g'][l], p['sgu_ln_b'][l])
    y_c = spatial_gate(uc, vn, p['sgu_w'][l], p['sgu_b'][l])
    mix = jnp.concatenate([y_a, y_b, y_c], axis=-1)
    x = x + jnp.einsum('blm,md->bld', mix, p['w_out'][l].astype(x.dtype))
    hf = rms_norm(x, p['norm_ffn'][l])
    ff = jnp.square(jax.nn.relu(jnp.einsum('bld,df->blf', hf, p['w_ff1'][l].astype(x.dtype))))
    x = x + jnp.einsum('blf,fd->bld', ff, p['w_ff2'][l].astype(x.dtype))
    return x, xa_full[:, -(K_A - 1):], h_last, ub_full[:, -(K_B - 1):], vn


def trunk(x, conv_a_bufs, h0s, conv_b_bufs, p):
    ca, hh, cb, vv = [], [], [], []
    for l in range(DEPTH):
        x, c_a, h_l, c_b, v_l = hybrid_layer(x, conv_a_bufs[l], h0s[l], conv_b_bufs[l], p, l)
        ca.append(c_a)
        hh.append(h_l)
        cb.append(c_b)
        vv.append(v_l)
    y = rms_norm(x, p['norm_final'])
    return y, jnp.stack(ca), jnp.stack(hh), jnp.stack(cb), jnp.stack(vv)


def setup_inputs(seed: int = 0) -> dict:
    key = jax.random.key(seed)
    ks = jax.random.split(key, 32)
    f32 = jnp.float32

    def nrm(k, shape, s):
        return jax.random.normal(k, shape, f32) * s

    p_a = jax.random.uniform(ks[10], (DEPTH, W_A), f32, minval=0.9, maxval=0.999)
    a0 = p_a ** (1.0 / LRU_C)
    lru_lambda = jnp.log(a0) - jnp.log1p(-a0)
    return {
        'x_prompt': nrm(ks[0], (BATCH, SEQ, D_MODEL), 1.0),
        'x_sample': nrm(ks[1], (DEC_BATCH, DEC_SEQ, D_MODEL), 1.0),
        'state_conv_a': nrm(ks[2], (DEPTH, DEC_BATCH, K_A - 1, W_A), 1.0),
        'state_lru_h': nrm(ks[3], (DEPTH, DEC_BATCH, W_A), 0.5),
        'state_conv_b': nrm(ks[4], (DEPTH, DEC_BATCH, K_B - 1, W_B), 1.0),
        'norm_mix': 1.0 + nrm(ks[5], (DEPTH, D_MODEL), 0.02),
        'w_in': nrm(ks[6], (DEPTH, D_MODEL, D_IN), D_MODEL ** -0.5),
        'conv_a_w': nrm(ks[7], (DEPTH, K_A, W_A), K_A ** -0.5),
        'conv_a_b': nrm(ks[8], (DEPTH, W_A), 0.01),
        'gate_r_w': nrm(ks[9], (DEPTH, H_A, HD_A, HD_A), HD_A ** -0.5),
        'gate_r_b': nrm(ks[11], (DEPTH, W_A), 0.01),
        'gate_i_w': nrm(ks[12], (DEPTH, H_A, HD_A, HD_A), HD_A ** -0.5),
        'gate_i_b': nrm(ks[13], (DEPTH, W_A), 0.01),
        'lru_lambda': lru_lambda,
        'conv_b_w': nrm(ks[14], (DEPTH, K_B, W_B), K_B ** -0.5),
        'ln_b_g': 1.0 + nrm(ks[15], (DEPTH, W_B), 0.02),
        'ln_b_b': nrm(ks[16], (DEPTH, W_B), 0.01),
        'sgu_ln_g': 1.0 + nrm(ks[17], (DEPTH, W_C), 0.02),
        'sgu_ln_b': nrm(ks[18], (DEPTH, W_C), 0.01),
        'sgu_w': nrm(ks[19], (DEPTH, H_C, CHUNK, CHUNK), CHUNK ** -0.5),
        'sgu_b': 1.0 + nrm(ks[20], (DEPTH, H_C, CHUNK), 0.02),
        'w_out': nrm(ks[21], (DEPTH, MIX_WIDTH, D_MODEL), MIX_WIDTH ** -0.5),
        'norm_ffn': 1.0 + nrm(ks[22], (DEPTH, D_MODEL), 0.02),
        'w_ff1': nrm(ks[23], (DEPTH, D_MODEL, D_FF), D_MODEL ** -0.5),
        'w_ff2': nrm(ks[24], (DEPTH, D_FF, D_MODEL), D_FF ** -0.5),
        'norm_final': 1.0 + nrm(ks[25], (D_MODEL,), 0.02),
    }


def reference(x_prompt, x_sample, state_conv_a, state_lru_h, state_conv_b,
              norm_mix, w_in, conv_a_w, conv_a_b, gate_r_w, gate_r_b, gate_i_w, gate_i_b,
              lru_lambda, conv_b_w, ln_b_g, ln_b_b, sgu_ln_g, sgu_ln_b, sgu_w, sgu_b,
              w_out, norm_ffn, w_ff1, w_ff2, norm_final):
    p = dict(norm_mix=norm_mix, w_in=w_in, conv_a_w=conv_a_w, conv_a_b=conv_a_b,
             gate_r_w=gate_r_w, gate_r_b=gate_r_b, gate_i_w=gate_i_w, gate_i_b=gate_i_b,
             lru_lambda=lru_lambda, conv_b_w=conv_b_w, ln_b_g=ln_b_g, ln_b_b=ln_b_b,
             sgu_ln_g=sgu_ln_g, sgu_ln_b=sgu_ln_b, sgu_w=sgu_w, sgu_b=sgu_b,
             w_out=w_out, norm_ffn=norm_ffn, w_ff1=w_ff1, w_ff2=w_ff2, norm_final=norm_final)
    nb = x_prompt.shape[0]
    zero_ca = jnp.zeros((DEPTH, nb, K_A - 1, W_A), x_prompt.dtype)
    zero_h = jnp.zeros((DEPTH, nb, W_A), jnp.float32)
    zero_cb = jnp.zeros((DEPTH, nb, K_B - 1, W_B), x_prompt.dtype)
    y_prompt, ca_p, h_p, cb_p, _ = trunk(x_prompt, zero_ca, zero_h, zero_cb, p)
    y_sample, ca_s, h_s, cb_s, v_s = trunk(x_sample, state_conv_a, state_lru_h, state_conv_b, p)
    return (y_prompt, y_sample, ca_p, h_p, cb_p, ca_s, h_s, cb_s, v_s)
```

```python
import numpy as np
from contextlib import ExitStack
import concourse.bass as bass
import concourse.mybir as mybir
from concourse.bass_utils import run_bass_kernel_spmd

_DEAD = [False]
F32 = mybir.dt.float32
BF16 = mybir.dt.bfloat16
AF = mybir.ActivationFunctionType
ALU = mybir.AluOpType

D = 2048
KD = 16
NP = 1024
NS = 16
TT = NP + NS
NW = 6
EPS = 1e-6
GELU_FUNC = AF.Gelu_apprx_tanh


class Rec:
    def __init__(self):
        self.calls = []

    def __getattr__(self, name):
        def f(*a, **kw):
            self.calls.append((name, a, kw))
            return self
        return f


def _record(fns):
    calls = []
    for f in fns:
        r = Rec()
        f(r)
        calls.extend(r.calls)
    return calls


def _replay(e, ops):
    for op in ops:
        if op[0] == 'wait':
            e.wait_ge(op[1], op[2])
        else:
            _, name, a, kw, sem, inc = op
            ins = getattr(e, name)(*a, **kw)
            if sem is not None:
                ins.then_inc(sem, inc)


class Chan:
    def __init__(self, sem):
        self.sem = sem
        self.val = 0


class Prog:
    def __init__(self):
        self.streams = {e: [] for e in ('pe', 'act', 'dve', 'pool', 'sp')}
        self.chan = {}
        self.known = {e: {} for e in self.streams}
        self.clock = {}
        self.lastw = {}
        self.readers = {}
        self.wslot = 0
        self.spc = 0

    def add_chan(self, name, sem):
        self.chan[name] = Chan(sem)

    def _deps(self, reads, writes):
        deps = {}

        def need(tok):
            if tok is None:
                return
            c, v = tok
            if deps.get(c, 0) < v:
                deps[c] = v
        for r in reads:
            need(self.lastw.get(r))
        for w in writes:
            need(self.lastw.get(w))
            for c, v in self.readers.get(w, {}).items():
                need((c, v))
        return deps

    def _emit_waits(self, eng, deps):
        kn = self.known[eng]
        ops = self.streams[eng]
        for c, v in deps.items():
            if eng == 'pe' and c == 'pe':
                continue
            if kn.get(c, 0) >= v:
                continue
            sem = self.chan[c].sem
            ops.append(('wait', sem, v))
            for c2, v2 in self.clock[(c, v)].items():
                if kn.get(c2, 0) < v2:
                    kn[c2] = v2

    def _record(self, tok, reads, writes):
        c, v = tok
        for r in reads:
            d = self.readers.setdefault(r, {})
            if d.get(c, 0) < v:
                d[c] = v
        for w in writes:
            self.lastw[w] = tok
            self.readers[w] = {}

    def unit(self, eng, fns, reads=(), writes=()):
        if _DEAD[0]:
            return
        if not isinstance(fns, (list, tuple)):
            fns = [fns]
        writes = list(writes) + [r for r in reads if r[0] == 'ps']
        reads = [r for r in reads if r[0] != 'ps']
        deps = self._deps(reads, writes)
        self._emit_waits(eng, deps)
        ch = self.chan[eng]
        ch.val += 1
        tok = (eng, ch.val)
        clk = dict(self.known[eng])
        clk[eng] = ch.val
        self.clock[tok] = clk
        ops = self.streams[eng]
        calls = _record(fns)
        for i, (name, a, kw) in enumerate(calls):
            lastc = (i == len(calls) - 1)
            ops.append(('call', name, a, kw, ch.sem if lastc else None, 1))
        self._record(tok, reads, writes)

    def dma(self, eng, fns, reads=(), writes=(), chan=None):
        if _DEAD[0]:
            return
        if not isinstance(fns, (list, tuple)):
            fns = [fns]
        if chan is None:
            chan = 'sp%d' % self.spc
            self.spc = (self.spc + 1) % 8
        deps = self._deps(reads, writes)
        ch = self.chan[chan]
        if ch.val > 0 and deps.get(chan, 0) < ch.val:
            deps[chan] = ch.val
        self._emit_waits(eng, deps)
        calls = _record(fns)
        ch.val += 16 * len(calls)
        tok = (chan, ch.val)
        clk = dict(self.known[eng])
        clk[chan] = ch.val
        self.clock[tok] = clk
        ops = self.streams[eng]
        for (name, a, kw) in calls:
            ops.append(('call', name, a, kw, ch.sem, 16))
        self._record(tok, reads, writes)

    def pe(self, fns, reads=(), writes=()):
        self.unit('pe', fns, reads, writes)

    def act(self, fns, reads=(), writes=()):
        self.unit('act', fns, reads, writes)

    def dve(self, fns, reads=(), writes=()):
        self.unit('dve', fns, reads, writes)

    def pool(self, fns, reads=(), writes=()):
        self.unit('pool', fns, reads, writes)

    def barrier(self, engines=('pe', 'act', 'dve', 'sp')):
        if _DEAD[0]:
            return
        for e in engines:
            deps = {}
            for c, ch in self.chan.items():
                if c.startswith('w'):
                    continue
                if ch.val > 0:
                    deps[c] = ch.val
            self._emit_waits(e, deps)
        for k in list(self.lastw.keys()):
            if k[0] != 'w':
                del self.lastw[k]
        for k in list(self.readers.keys()):
            if k[0] != 'w':
                del self.readers[k]

    def final_wait(self, eng='sp'):
        deps = {c: ch.val for c, ch in self.chan.items() if ch.val > 0}
        self._emit_waits(eng, deps)


import os as _os


class _Stop(Exception):
    pass


def sub(n):
    if int(_os.environ.get('KSUB', '0')) == n:
        _DEAD[0] = True


def build_nc():
    nc = bass.Bass("TRN2", target_bir_lowering=False)

    def din(name, shape):
        return nc.dram_tensor(name, list(shape), F32, kind="ExternalInput").ap()

    def dout(name, shape):
        return nc.dram_tensor(name, list(shape), F32, kind="ExternalOutput").ap()

    xp = din("xp", [2048, D])
    xs = din("xs", [NS, D])
    sca = din("sca", [2, NS, 3, 1024])
    slh = din("slh", [2, NS, 1024])
    scb = din("scb", [2, NS, 30, 512])
    norm_mix = din("norm_mix", [2, D])
    w_in = din("w_in", [2, D, 4096])
    conv_a_w = din("conv_a_w", [2, 4, 1024])
    conv_a_b = din("conv_a_b", [2, 1024])
    gate_r_w = din("gate_r_w", [2, 8, 128, 128])
    gate_r_b = din("gate_r_b", [2, 1024])
    gate_i_w = din("gate_i_w", [2, 8, 128, 128])
    gate_i_b = din("gate_i_b", [2, 1024])
    lru_lambda = din("lru_lambda", [2, 1024])
    conv_b_w = din("conv_b_w", [2, 31, 512])
    ln_b_g = din("ln_b_g", [2, 512])
    ln_b_b = din("ln_b_b", [2, 512])
    sgu_ln_g = din("sgu_ln_g", [2, 512])
    sgu_ln_b = din("sgu_ln_b", [2, 512])
    sgu_w = din("sgu_w", [2, 4, 128, 128])
    sgu_b = din("sgu_b", [2, 4, 128])
    w_out = din("w_out", [2, D, D])
    norm_ffn = din("norm_ffn", [2, D])
    w_ff1 = din("w_ff1", [2, D, 8192])
    w_ff2 = din("w_ff2", [2, 8192, D])
    norm_final = din("norm_final", [D])

    yp = dout("yp", [2048, D])
    ys = dout("ys", [NS, D])
    cap = dout("cap", [2, 3, 1024])
    hp = dout("hp", [2, 1024])
    cbp = dout("cbp", [2, 30, 512])
    cas = dout("cas", [2, NS, 3, 1024])
    hs = dout("hs", [2, NS, 1024])
    cbs = dout("cbs", [2, NS, 30, 512])
    vs = dout("vs", [2, NS, 512])

    P = Prog()
    uid = [0]
    es = ExitStack()
    with es:
        def sb_(name, shape, dt=F32):
            return es.enter_context(nc.sbuf_tensor(name, list(shape), dt))

        def tmp(ph, name, shape, dt=F32):
            uid[0] += 1
            return ph.enter_context(nc.sbuf_tensor("%s_u%d" % (name, uid[0]), list(shape), dt))

        def sem_(name):
            return es.enter_context(nc.semaphore(name))

        for e in ('pe', 'act', 'dve', 'pool'):
            P.add_chan(e, sem_("s_" + e))
        for i in range(8):
            P.add_chan('sp%d' % i, sem_("s_sp%d" % i))
        for i in range(NW):
            P.add_chan('w%d' % i, sem_("s_w%d" % i))
        for i in range(2):
            P.add_chan('wg%d' % i, sem_("s_wg%d" % i))

        with nc.sbuf_tensor("zinit", [128, 3, 17600], F32) as zz:
            P.dve(lambda e: e.memset(zz[:, 0, :], 0.0), writes=[('zz', 0)])
            P.dve(lambda e: e.memset(zz[:, 1, :], 0.0), writes=[('zz', 1)])
            P.pool(lambda e: e.memset(zz[:, 2, :], 0.0), writes=[('zz', 2)])
        P.barrier(engines=('pe', 'act', 'dve', 'sp', 'pool'))
        xT = sb_("xT", [128, KD, TT], F32)
        hT = sb_("hT", [128, KD, TT], BF16)
        mixT = sb_("mixT", [128, KD, TT], BF16)
        wring = [sb_("wr%d" % i, [128, KD, 128], BF16) for i in range(NW)]
        wg = [sb_("wg%d" % i, [128, 2, 128], BF16) for i in range(2)]
        ident = sb_("ident", [128, 128], F32)
        identb = sb_("identb", [128, 128], BF16)
        onesb = sb_("onesb", [128, 128], BF16)
        onesf = sb_("onesf", [128, 128], F32)
        tri = sb_("tri", [128, 128], F32)
        epsT = sb_("epsT", [128, 1], F32)
        oneT = sb_("oneT", [128, 1], F32)
        cT = sb_("cT", [128, 104], F32)
        cbw = sb_("cbw", [128, 124], F32)
        gfT = sb_("gfT", [128, 16], F32)
        c1 = sb_("c1", [128, 8], F32)
        c2 = sb_("c2", [128, 8], F32)
        caT = sb_("caT", [128, 2, 8, 3], F32)
        cbT = sb_("cbT", [128, 2, 4, 30], F32)
        hcT = sb_("hcT", [128, 2, 8], F32)
        SaT = sb_("SaT", [128, 8, 48], BF16)
        SbT = sb_("SbT", [128, 4, 480], BF16)
        h0T = sb_("h0T", [128, 8, 16], F32)
        xasF = sb_("xasF", [128, 8, 16], F32)
        ubsF = sb_("ubsF", [128, 4, 16], F32)
        hsF = sb_("hsF", [128, 8, 16], F32)
        scr = sb_("scr", [128, 8], F32)

        ps = [es.enter_context(nc.psum_tensor("ps%d" % i, [128, 512], F32)) for i in range(8)]
        SA = [0, 1, 2]
        SB = [3, 4, 5]

        def xres(ks=range(KD)):
            return [('xT', k) for k in ks]

        hres = [('hT', k) for k in range(KD)]
        mres = [('mixT', k) for k in range(KD)]

        def load_w(src_ap):
            s = P.wslot
            P.wslot = (s + 1) % NW
            P.dma('pool', [lambda e: e.dma_start(out=wring[s][:], in_=src_ap)],
                  writes=[('w', s)], chan='w%d' % s)
            return s

        def wview(wfull, r0, c0):
            return wfull[r0:r0 + 2048, c0:c0 + 128].rearrange("(k p) c -> p k c", p=128)

        bset_state = [0]

        def next_bset():
            bset_state[0] ^= 1
            return SA if bset_state[0] else SB

        def mm_w(slot, src, src_res, bset, tl):
            wt = wring[slot]
            fns = []
            for k in range(KD):
                for ti, (c0, n) in enumerate(tl):
                    fns.append(lambda e, ti=ti, c0=c0, n=n, k=k: e.matmul(
                        ps[bset[ti]][:, 0:n], lhsT=wt[:, k, :], rhs=src[:, k, c0:c0 + n],
                        start=(k == 0), stop=(k == KD - 1)))
            P.pe(fns, reads=[('w', slot)] + list(src_res), writes=[('ps', bset[ti]) for ti in range(len(tl))])

        def tr(in_ap, n_in_part, out_ap):
            return lambda e: e.transpose(out=out_ap, in_=in_ap, identity=ident[0:n_in_part, 0:n_in_part])

        P.pool(lambda e: e.memset(onesf[:], 1.0), writes=[('onesf',)])
        P.pool(lambda e: e.memset(epsT[:], EPS), writes=[('epsT',)])
        P.pool(lambda e: e.memset(oneT[:], 1.0), writes=[('oneT',)])
        P.pool(lambda e: e.affine_select(out=ident[:], in_=onesf[:], pattern=[[1, 128]], compare_op=ALU.is_equal,
                                         fill=0.0, base=0, channel_multiplier=-1),
               reads=[('onesf',)], writes=[('ident',)])
        P.pool(lambda e: e.affine_select(out=tri[:], in_=onesf[:], pattern=[[1, 128]], compare_op=ALU.is_ge,
                                         fill=0.0, base=0, channel_multiplier=-1),
               reads=[('onesf',)], writes=[('tri',)])
        P.dve(lambda e: e.tensor_copy(out=identb[:], in_=ident[:]), reads=[('ident',)], writes=[('identb',)])
        P.dve(lambda e: e.tensor_copy(out=onesb[:], in_=onesf[:]), reads=[('onesf',)], writes=[('onesb',)])
        P.dve(lambda e: e.memset(caT[:], 0.0), writes=[('caT',)])
        P.dve(lambda e: e.memset(cbT[:], 0.0), writes=[('cbT',)])
        P.dve(lambda e: e.memset(hcT[:], 0.0), writes=[('hcT',)])
        with ExitStack() as ph:
            gst = tmp(ph, "gst", [16, 128], F32)
            P.dma('sp', [lambda e: e.dma_start(out=gst[:], in_=norm_final.rearrange("(n c) -> n c", c=128))],
                  writes=[('gst',)])
            P.pe([tr(gst[0:16, :], 16, ps[6][:, 0:16])], reads=[('gst',), ('ident',)], writes=[('ps', 6)])
            P.dve(lambda e: e.tensor_copy(out=gfT[:], in_=ps[6][:, 0:16]), reads=[('ps', 6)], writes=[('gfT',)])
        P.barrier()

        def rmsnorm(gcol, to_hT, tl, TS):
            with ExitStack() as ph:
                sq = [tmp(ph, "sq", [128, TT], BF16) for i in range(2)]
                rstd = tmp(ph, "rstd", [128, TT], F32)
                for k in range(KD):
                    b = k % 2
                    P.act(lambda e: e.activation(out=sq[b][:, 0:TS], in_=xT[:, k, 0:TS], func=AF.Square),
                          reads=[('xT', k)], writes=[('sq', b)])
                    fns = [lambda e, ti=ti, c0=c0, n=n: e.matmul(
                        ps[SA[ti]][:, 0:n], lhsT=onesb[:], rhs=sq[b][:, c0:c0 + n],
                        start=(k == 0), stop=(k == KD - 1)) for ti, (c0, n) in enumerate(tl)]
                    P.pe(fns, reads=[('sq', b), ('onesb',)], writes=[('ps', SA[ti]) for ti in range(len(tl))])
                for ti, (c0, n) in enumerate(tl):
                    P.act(lambda e: e.activation(out=rstd[:, c0:c0 + n], in_=ps[SA[ti]][:, 0:n], func=AF.Sqrt,
                                                 bias=epsT[:, 0:1], scale=1.0 / D),
                          reads=[('ps', SA[ti]), ('epsT',)], writes=[('rstd', ti)])
                    P.dve(lambda e: e.reciprocal(out=rstd[:, c0:c0 + n], in_=rstd[:, c0:c0 + n]),
                          reads=[('rstd', ti)], writes=[('rstd', ti)])
                rr = [('rstd', ti) for ti in range(len(tl))]
                for k in range(KD):
                    dst = hT if to_hT else xT
                    P.dve(lambda e: e.scalar_tensor_tensor(
                        out=dst[:, k, 0:TS], in0=xT[:, k, 0:TS], scalar=gcol(k), in1=rstd[:, 0:TS],
                        op0=ALU.mult, op1=ALU.mult),
                        reads=[('xT', k), ('cT',), ('gfT',)] + rr, writes=[('hT', k) if to_hT else ('xT', k)])
            P.barrier()

        def do_consts(sb, l, has_s):
            with ExitStack() as ph:
                cst = tmp(ph, "cst", [128, 128], F32)
                cst2 = tmp(ph, "cst2", [128, 128], F32)
                vecs = [(conv_a_b[l], 8), (gate_r_b[l], 8), (gate_i_b[l], 8), (lru_lambda[l], 8),
                        (ln_b_g[l], 4), (ln_b_b[l], 4), (norm_mix[l], 16), (norm_ffn[l], 16)]
                fns = []
                r = 0
                for v, n in vecs:
                    fns.append(lambda e, v=v, n=n, r=r: e.dma_start(
                        out=cst[r:r + n, :], in_=v.rearrange("(n c) -> n c", c=128)))
                    r += n
                fns.append(lambda e: e.dma_start(out=cst[72:104, :],
                                                 in_=conv_a_w[l].rearrange("k (j c) -> (k j) c", c=128)))
                fns.append(lambda e: e.dma_start(out=cst2[0:124, :],
                                                 in_=conv_b_w[l].rearrange("k (j c) -> (k j) c", c=128)))
                P.dma('sp', fns, writes=[('cst',), ('cst2',)])
                P.pe([tr(cst[0:104, :], 104, ps[6][:, 0:104])], reads=[('cst',), ('ident',)], writes=[('ps', 6)])
                P.dve(lambda e: e.tensor_copy(out=cT[:], in_=ps[6][:, 0:104]), reads=[('ps', 6)], writes=[('cT',)])
                P.pe([tr(cst2[0:124, :], 124, ps[7][:, 0:124])], reads=[('cst2',), ('ident',)], writes=[('ps', 7)])
                P.dve(lambda e: e.tensor_copy(out=cbw[:], in_=ps[7][:, 0:124]), reads=[('ps', 7)], writes=[('cbw',)])
                P.act(lambda e: e.activation(out=scr[:], in_=cT[:, 24:32], func=AF.Exp, scale=-1.0),
                      reads=[('cT',)], writes=[('scr',)])
                P.act(lambda e: e.activation(out=scr[:], in_=scr[:], func=AF.Ln, bias=oneT[:, 0:1], scale=1.0),
                      reads=[('scr',), ('oneT',)], writes=[('scr',)])
                P.dve(lambda e: e.tensor_scalar_mul(out=c1[:], in0=scr[:], scalar1=-8.0),
                      reads=[('scr',)], writes=[('c1',)])
                P.dve(lambda e: e.tensor_scalar_mul(out=c2[:], in0=scr[:], scalar1=-16.0),
                      reads=[('scr',)], writes=[('c2',)])
                if has_s:
                    sst = tmp(ph, "sst", [128, 4, 512], F32)
                    P.dma('sp', [lambda e: e.dma_start(out=sst[0:48, 0:2, :].rearrange("p a c -> p (a c)"),
                                                       in_=sca[l].rearrange("b k c -> (b k) c"))],
                          writes=[('sst',)])
                    for q in range(2):
                        bk = 6 + q
                        fns = [tr(sst[0:48, (4 * q + i) // 4, ((4 * q + i) % 4) * 128:((4 * q + i) % 4 + 1) * 128],
                                  48, ps[bk][:, i * 48:(i + 1) * 48]) for i in range(4)]
                        P.pe(fns, reads=[('sst',), ('ident',)], writes=[('ps', bk)])
                        P.dve(lambda e: e.tensor_copy(out=SaT[:, 4 * q:4 * q + 4, :],
                                                      in_=ps[bk][:, 0:192].rearrange("p (i n) -> p i n", n=48)),
                              reads=[('ps', bk)], writes=[('SaT',)])
                    P.dma('sp', [lambda e: e.dma_start(out=sst[0:16, 0:2, :].rearrange("p a c -> p (a c)"), in_=slh[l])],
                          writes=[('sst',)])
                    fns = [tr(sst[0:16, j // 4, (j % 4) * 128:(j % 4 + 1) * 128], 16,
                              ps[6][:, j * 16:(j + 1) * 16]) for j in range(8)]
                    P.pe(fns, reads=[('sst',), ('ident',)], writes=[('ps', 6)])
                    P.dve(lambda e: e.tensor_copy(out=h0T[:], in_=ps[6][:, 0:128].rearrange("p (j n) -> p j n", n=16)),
                          reads=[('ps', 6)], writes=[('h0T',)])
                    P.dma('sp', [lambda e: e.dma_start(out=sst[0:120, :, :],
                                                       in_=scb[l].rearrange("(g b) k c -> (b k) g c", b=4))],
                          writes=[('sst',)])
                    for j in range(4):
                        bk = 6 + (j % 2)
                        fns = [tr(sst[0:120, g, j * 128:(j + 1) * 128], 120,
                                  ps[bk][:, g * 120:(g + 1) * 120]) for g in range(4)]
                        P.pe(fns, reads=[('sst',), ('ident',)], writes=[('ps', bk)])
                        P.dve(lambda e: e.tensor_copy(out=SbT[:, j, :], in_=ps[bk][:, 0:480]),
                              reads=[('ps', bk)], writes=[('SbT',)])
                    P.dma('sp', [lambda e: e.dma_start(out=cas[l][:, 0:2, :], in_=sca[l][:, 1:3, :]),
                                 lambda e: e.dma_start(out=cbs[l][:, 0:29, :], in_=scb[l][:, 1:30, :])],
                          writes=[('o_shift', l)])
            P.barrier()

        def mixer_a(sb, l, tl, TS, has_s):
            with ExitStack() as ph:
                xes = [tmp(ph, "xa_ext", [128, 3 + NP], BF16) for i in range(2)]
                xasbs = [tmp(ph, "xasb", [128, NS], BF16) for i in range(2)]
                ggs = [tmp(ph, "gg", [128, TT], BF16) for i in range(2)]
                xc = tmp(ph, "xc", [128, TT], F32)
                xcb = tmp(ph, "xcb", [128, TT], BF16)
                rT = tmp(ph, "rT", [128, TT], F32)
                iT = tmp(ph, "iT", [128, TT], F32)
                aT = tmp(ph, "aT", [128, TT], F32)
                dgA = tmp(ph, "dgA", [128, 4, 128], BF16)
                hh = xc
                nt = len(tl)
                allr = [('rT', ti) for ti in range(nt)]
                alli = [('iT', ti) for ti in range(nt)]
                allx = [('xc', ti) for ti in range(nt)]
                slots = {}

                def preA(j):
                    b = j % 2
                    xe = xes[b]
                    P.dma('pool', [lambda e: e.dma_start(out=wg[b][:, 0, :], in_=gate_r_w[l, j]),
                                   lambda e: e.dma_start(out=wg[b][:, 1, :], in_=gate_i_w[l, j])],
                          writes=[('w', 'g', b)], chan='wg%d' % b)
                    s_xa = load_w(wview(w_in[l], 0, j * 128))
                    slots[j] = load_w(wview(w_in[l], 0, 1024 + j * 128))
                    bs1 = next_bset()
                    mm_w(s_xa, hT, hres, bs1, tl)
                    P.dve(lambda e: e.tensor_copy(out=xe[:, 0:3], in_=caT[:, l, j, :]),
                          reads=[('caT',)], writes=[('xa_ext', b)])
                    for ti, (c0, n) in enumerate(tl[:2]):
                        P.act(lambda e: e.activation(out=xe[:, 3 + c0:3 + c0 + n], in_=ps[bs1[ti]][:, 0:n], func=AF.Copy),
                              reads=[('ps', bs1[ti])], writes=[('xa_ext', b)])
                    P.act(lambda e: e.activation(out=caT[:, l, j, :], in_=ps[bs1[1]][:, 509:512], func=AF.Copy),
                          reads=[('ps', bs1[1])], writes=[('caT',)])
                    if has_s:
                        P.act(lambda e: e.activation(out=xasF[:, j, :], in_=ps[bs1[2]][:, 0:NS], func=AF.Copy),
                              reads=[('ps', bs1[2])], writes=[('xasF',)])
                        P.act(lambda e: e.activation(out=xasbs[b][:], in_=ps[bs1[2]][:, 0:NS], func=AF.Copy),
                              reads=[('ps', bs1[2])], writes=[('xasb', b)])

                def preB(j):
                    b = j % 2
                    bs2 = next_bset()
                    mm_w(slots[j], hT, hres, bs2, tl)
                    for ti, (c0, n) in enumerate(tl):
                        P.act(lambda e: e.activation(out=ggs[b][:, c0:c0 + n], in_=ps[bs2[ti]][:, 0:n], func=GELU_FUNC),
                              reads=[('ps', bs2[ti])], writes=[('gg', b)])

                def conv(j, ti):
                    b = j % 2
                    xe = xes[b]
                    c0, n = tl[ti]
                    if ti == 0:
                        for k in range(4):
                            P.dve(lambda e: e.tensor_scalar_mul(out=dgA[:, k, :], in0=identb[:],
                                                                scalar1=cT[:, 72 + k * 8 + j:73 + k * 8 + j]),
                                  reads=[('identb',), ('cT',)], writes=[('dgA',)])
                    if ti < 2:
                        fns = [lambda e, k=k: e.matmul(ps[6][:, 0:n], lhsT=dgA[:, k, :], rhs=xe[:, c0 + k:c0 + k + n],
                                                       start=(k == 0), stop=(k == 3)) for k in range(4)]
                        P.pe(fns, reads=[('dgA',), ('xa_ext', b)], writes=[('ps', 6)])
                    else:
                        fns = [lambda e: e.matmul(ps[6][:, 0:NS], lhsT=dgA[:, 3, :], rhs=xasbs[b][:], start=True, stop=False)]
                        for k in range(3):
                            fns.append(lambda e, k=k: e.matmul(ps[6][:, 0:NS], lhsT=dgA[:, k, :], rhs=SaT[:, j, k:48:3],
                                                               start=False, stop=(k == 2)))
                        P.pe(fns, reads=[('dgA',), ('xasb', b), ('SaT',)], writes=[('ps', 6)])
                    P.dve(lambda e: e.tensor_scalar_add(out=xc[:, c0:c0 + n], in0=ps[6][:, 0:n], scalar1=cT[:, j:j + 1]),
                          reads=[('ps', 6), ('cT',)], writes=[('xc', ti)])
                    P.act(lambda e: e.activation(out=xcb[:, c0:c0 + n], in_=ps[6][:, 0:n], func=AF.Identity,
                                                 bias=cT[:, j:j + 1], scale=1.0),
                          reads=[('ps', 6), ('cT',)], writes=[('xcb', ti)])

                def gates(j, ti):
                    b = j % 2
                    c0, n = tl[ti]
                    P.pe([lambda e: e.matmul(ps[7][:, 0:n], lhsT=wg[b][:, 0, :], rhs=xcb[:, c0:c0 + n], start=True, stop=True)],
                         reads=[('w', 'g', b), ('xcb', ti)], writes=[('ps', 7)])
                    P.act(lambda e: e.activation(out=rT[:, c0:c0 + n], in_=ps[7][:, 0:n], func=AF.Sigmoid,
                                                 bias=cT[:, 8 + j:9 + j], scale=1.0),
                          reads=[('ps', 7), ('cT',)], writes=[('rT', ti)])
                    P.pe([lambda e: e.matmul(ps[7][:, 0:n], lhsT=wg[b][:, 1, :], rhs=xcb[:, c0:c0 + n], start=True, stop=True)],
                         reads=[('w', 'g', b), ('xcb', ti)], writes=[('ps', 7)])
                    P.act(lambda e: e.activation(out=iT[:, c0:c0 + n], in_=ps[7][:, 0:n], func=AF.Sigmoid,
                                                 bias=cT[:, 16 + j:17 + j], scale=1.0),
                          reads=[('ps', 7), ('cT',)], writes=[('iT', ti)])

                def tail(j):
                    b = j % 2
                    gg = ggs[b]
                    P.act(lambda e: e.activation(out=aT[:, 0:TS], in_=rT[:, 0:TS], func=AF.Exp, scale=c1[:, j:j + 1]),
                          reads=allr + [('c1',)], writes=[('aT',)])
                    P.act(lambda e: e.activation(out=rT[:, 0:TS], in_=rT[:, 0:TS], func=AF.Exp, scale=c2[:, j:j + 1]),
                          reads=allr + [('c2',)], writes=allr)
                    P.act(lambda e: e.activation(out=rT[:, 0:TS], in_=rT[:, 0:TS], func=AF.Sqrt, bias=oneT[:, 0:1], scale=-1.0),
                          reads=allr + [('oneT',)], writes=allr)
                    P.dve(lambda e: e.tensor_tensor(out=iT[:, 0:TS], in0=iT[:, 0:TS], in1=xc[:, 0:TS], op=ALU.mult),
                          reads=alli + allx, writes=alli)
                    P.dve(lambda e: e.tensor_tensor(out=iT[:, 0:TS], in0=iT[:, 0:TS], in1=rT[:, 0:TS], op=ALU.mult),
                          reads=alli + allr, writes=alli)
                    P.dve(lambda e: e.tensor_tensor_scan(out=hh[:, 0:NP], data0=aT[:, 0:NP], data1=iT[:, 0:NP],
                                                         initial=hcT[:, l, j:j + 1], op0=ALU.mult, op1=ALU.add),
                          reads=[('aT',), ('hcT',)] + alli, writes=allx)
                    P.dve(lambda e: e.tensor_copy(out=hcT[:, l, j:j + 1], in_=hh[:, NP - 1:NP]),
                          reads=allx, writes=[('hcT',)])
                    if has_s:
                        P.dve(lambda e: e.tensor_tensor(out=hh[:, NP:TT], in0=aT[:, NP:TT], in1=h0T[:, j, :], op=ALU.mult),
                              reads=[('aT',), ('h0T',)], writes=allx)
                        P.dve(lambda e: e.tensor_tensor(out=hh[:, NP:TT], in0=hh[:, NP:TT], in1=iT[:, NP:TT], op=ALU.add),
                              reads=allx + alli, writes=allx)
                        P.dve(lambda e: e.tensor_copy(out=hsF[:, j, :], in_=hh[:, NP:TT]),
                              reads=allx, writes=[('hsF',)])
                    P.dve(lambda e: e.tensor_tensor(out=mixT[:, j, 0:TS], in0=hh[:, 0:TS], in1=gg[:, 0:TS], op=ALU.mult),
                          reads=allx + [('gg', b)], writes=[('mixT', j)])

                preA(0)
                preB(0)
                for j in range(8):
                    conv(j, 0)
                    if j < 7:
                        preA(j + 1)
                    gates(j, 0)
                    conv(j, 1)
                    if j < 7:
                        preB(j + 1)
                    gates(j, 1)
                    if has_s:
                        conv(j, 2)
                        gates(j, 2)
                    tail(j)
            P.barrier()

        def mixer_b(sb, l, tl, TS, has_s):
            with ExitStack() as ph:
                ub_ext = tmp(ph, "ub_ext", [128, 30 + NP], BF16)
                ubsb = tmp(ph, "ubsb", [128, NS], BF16)
                sg = tmp(ph, "sg", [128, 512], F32)
                yb = tmp(ph, "yb", [128, 4, TT], F32)
                dgB = tmp(ph, "dgB", [128, 31, 128], BF16)
                ysq = [tmp(ph, "ysq", [128, 512], F32) for i in range(2)]
                mean = tmp(ph, "mean", [128, 512], F32)
                rstd = tmp(ph, "rstdb", [128, 512], F32)
                for j in range(4):
                    for k in range(31):
                        P.dve(lambda e: e.tensor_scalar_mul(out=dgB[:, k, :], in0=identb[:],
                                                            scalar1=cbw[:, k * 4 + j:k * 4 + j + 1]),
                              reads=[('identb',), ('cbw',)], writes=[('dgB',)])
                    s_xb = load_w(wview(w_in[l], 0, 2048 + j * 128))
                    s_gb = load_w(wview(w_in[l], 0, 2560 + j * 128))
                    bs1 = next_bset()
                    mm_w(s_xb, hT, hres, bs1, tl)
                    bs2 = next_bset()
                    mm_w(s_gb, hT, hres, bs2, tl)
                    P.dve(lambda e: e.tensor_copy(out=ub_ext[:, 0:30], in_=cbT[:, l, j, :]),
                          reads=[('cbT',)], writes=[('ub_ext',)])
                    for ti, (c0, n) in enumerate(tl):
                        P.act(lambda e: e.activation(out=sg[:, 0:n], in_=ps[bs2[ti]][:, 0:n], func=AF.Sigmoid),
                              reads=[('ps', bs2[ti])], writes=[('sg',)])
                        if ti < 2:
                            P.dve(lambda e: e.tensor_tensor(out=ub_ext[:, 30 + c0:30 + c0 + n], in0=ps[bs1[ti]][:, 0:n],
                                                            in1=sg[:, 0:n], op=ALU.mult),
                                  reads=[('ps', bs1[ti]), ('sg',)], writes=[('ub_ext',)])
                            if ti == 1:
                                P.dve(lambda e: e.tensor_tensor(out=cbT[:, l, j, :], in0=ps[bs1[1]][:, 482:512],
                                                                in1=sg[:, 482:512], op=ALU.mult),
                                      reads=[('ps', bs1[1]), ('sg',)], writes=[('cbT',)])
                        else:
                            P.dve(lambda e: e.tensor_tensor(out=ubsF[:, j, :], in0=ps[bs1[2]][:, 0:NS], in1=sg[:, 0:NS],
                                                            op=ALU.mult),
                                  reads=[('ps', bs1[2]), ('sg',)], writes=[('ubsF',)])
                            P.dve(lambda e: e.tensor_copy(out=ubsb[:], in_=ubsF[:, j, :]),
                                  reads=[('ubsF',)], writes=[('ubsb',)])
                    for ti, (c0, n) in enumerate(tl):
                        bk = 6 + (ti % 2)
                        if ti < 2:
                            fns = [lambda e, k=k: e.matmul(ps[bk][:, 0:n], lhsT=dgB[:, k, :], rhs=ub_ext[:, c0 + k:c0 + k + n],
                                                           start=(k == 0), stop=(k == 30)) for k in range(31)]
                            P.pe(fns, reads=[('dgB',), ('ub_ext',)], writes=[('ps', bk)])
                        else:
                            fns = [lambda e: e.matmul(ps[bk][:, 0:NS], lhsT=dgB[:, 30, :], rhs=ubsb[:], start=True, stop=False)]
                            for k in range(30):
                                fns.append(lambda e, k=k: e.matmul(ps[bk][:, 0:NS], lhsT=dgB[:, k, :], rhs=SbT[:, j, k:480:30],
                                                                   start=False, stop=(k == 29)))
                            P.pe(fns, reads=[('dgB',), ('ubsb',), ('SbT',)], writes=[('ps', bk)])
                        P.act(lambda e: e.activation(out=yb[:, j, c0:c0 + n], in_=ps[bk][:, 0:n], func=AF.Copy),
                              reads=[('ps', bk)], writes=[('yb', j, ti)])
                for ti, (c0, n) in enumerate(tl):
                    for j in range(4):
                        yq = ysq[j % 2]
                        P.act(lambda e: e.activation(out=yq[:, 0:n], in_=yb[:, j, c0:c0 + n], func=AF.Square),
                              reads=[('yb', j, ti)], writes=[('ysq', j % 2)])
                        P.pe([lambda e: e.matmul(ps[6][:, 0:n], lhsT=onesf[:], rhs=yb[:, j, c0:c0 + n],
                                                 start=(j == 0), stop=(j == 3)),
                              lambda e: e.matmul(ps[7][:, 0:n], lhsT=onesf[:], rhs=yq[:, 0:n],
                                                 start=(j == 0), stop=(j == 3))],
                             reads=[('yb', j, ti), ('ysq', j % 2), ('onesf',)], writes=[('ps', 6), ('ps', 7)])
                    P.act(lambda e: e.activation(out=mean[:, 0:n], in_=ps[6][:, 0:n], func=AF.Copy, scale=1.0 / 512),
                          reads=[('ps', 6)], writes=[('mean',)])
                    P.dve(lambda e: e.tensor_tensor(out=rstd[:, 0:n], in0=mean[:, 0:n], in1=mean[:, 0:n], op=ALU.mult),
                          reads=[('mean',)], writes=[('rstdb',)])
                    P.dve(lambda e: e.scalar_tensor_tensor(out=rstd[:, 0:n], in0=ps[7][:, 0:n], scalar=1.0 / 512,
                                                           in1=rstd[:, 0:n], op0=ALU.mult, op1=ALU.subtract),
                          reads=[('ps', 7), ('rstdb',)], writes=[('rstdb',)])
                    P.act(lambda e: e.activation(out=rstd[:, 0:n], in_=rstd[:, 0:n], func=AF.Sqrt, bias=epsT[:, 0:1], scale=1.0),
                          reads=[('rstdb',), ('epsT',)], writes=[('rstdb',)])
                    P.dve(lambda e: e.reciprocal(out=rstd[:, 0:n], in_=rstd[:, 0:n]),
                          reads=[('rstdb',)], writes=[('rstdb',)])
                    for j in range(4):
                        P.dve(lambda e: e.tensor_tensor(out=yb[:, j, c0:c0 + n], in0=yb[:, j, c0:c0 + n], in1=mean[:, 0:n],
                                                        op=ALU.subtract),
                              reads=[('yb', j, ti), ('mean',)], writes=[('yb', j, ti)])
                        P.dve(lambda e: e.tensor_tensor(out=yb[:, j, c0:c0 + n], in0=yb[:, j, c0:c0 + n], in1=rstd[:, 0:n],
                                                        op=ALU.mult),
                              reads=[('yb', j, ti), ('rstdb',)], writes=[('yb', j, ti)])
                        P.act(lambda e: e.activation(out=mixT[:, 8 + j, c0:c0 + n], in_=yb[:, j, c0:c0 + n], func=AF.Silu,
                                                     bias=cT[:, 36 + j:37 + j], scale=cT[:, 32 + j:33 + j]),
                              reads=[('yb', j, ti), ('cT',)], writes=[('mixT', 8 + j)])
            P.barrier()

        def mixer_c(sb, l, tl, TS, has_s):
            with ExitStack() as ph:
                ucg = tmp(ph, "ucg", [128, 4, TT], BF16)
                vg = [tmp(ph, "vg", [128, 512], F32) for i in range(2)]
                vnb = [tmp(ph, "vnb", [128, 512], BF16) for i in range(2)]
                st6 = tmp(ph, "st6", [128, 6], F32)
                mv = tmp(ph, "mv", [128, 2], F32)
                tmpc = tmp(ph, "tmpc", [128, 128], F32)
                sgs = tmp(ph, "sgs", [128, 4, 128], F32)
                Wt = tmp(ph, "Wt", [128, 4, 128], BF16)
                bsb = tmp(ph, "bsb", [128, 4, 128], F32)
                gbc = tmp(ph, "gbc", [128, 512], F32)
                bbc = tmp(ph, "bbc", [128, 512], F32)
                w00T = tmp(ph, "w00T", [128, 4], F32)
                b0T = tmp(ph, "b0T", [128, 4], F32)
                rhs_s = tmp(ph, "rhs_s", [16, 4, 16], BF16)
                fns = [lambda e: e.dma_start(out=sgs[:], in_=sgu_w[l].rearrange("h t s -> t h s")),
                       lambda e: e.dma_start(out=bsb[:], in_=sgu_b[l].partition_broadcast(128)),
                       lambda e: e.dma_start(out=gbc[:], in_=sgu_ln_g[l].partition_broadcast(128)),
                       lambda e: e.dma_start(out=bbc[:], in_=sgu_ln_b[l].partition_broadcast(128))]
                for h in range(4):
                    fns.append(lambda e, h=h: e.dma_start(out=w00T[:, h:h + 1], in_=sgu_w[l, h, 0, 0:1].partition_broadcast(128)))
                    fns.append(lambda e, h=h: e.dma_start(out=b0T[:, h:h + 1], in_=sgu_b[l, h, 0:1].partition_broadcast(128)))
                P.dma('sp', fns, writes=[('sgs',), ('bsb',), ('gbc',), ('bbc',), ('w00T',), ('b0T',)])
                for h in range(4):
                    bk = 6 + (h % 2)
                    P.pe([tr(sgs[:, h, :], 128, ps[bk][:, 0:128])], reads=[('sgs',), ('ident',)], writes=[('ps', bk)])
                    P.dve(lambda e: e.tensor_tensor(out=Wt[:, h, :], in0=ps[bk][:, 0:128], in1=tri[:], op=ALU.mult),
                          reads=[('ps', bk), ('tri',)], writes=[('Wt',)])
                    P.dve(lambda e: e.tensor_scalar_mul(out=rhs_s[:, h, :], in0=ident[0:16, 0:16], scalar1=w00T[0:16, h:h + 1]),
                          reads=[('w00T',), ('ident',)], writes=[('rhs_s',)])
                for j in range(4):
                    s_uc = load_w(wview(w_in[l], 0, 3072 + j * 128))
                    bs1 = next_bset()
                    mm_w(s_uc, hT, hres, bs1, tl)
                    for ti, (c0, n) in enumerate(tl):
                        P.act(lambda e: e.activation(out=ucg[:, j, c0:c0 + n], in_=ps[bs1[ti]][:, 0:n], func=GELU_FUNC),
                              reads=[('ps', bs1[ti])], writes=[('ucg', j)])
                s_vc = [load_w(wview(w_in[l], 0, 3584 + j * 128)) for j in range(4)]
                ttl = [(t * 128, 128) for t in range(8)] + ([(NP, NS)] if has_s else [])
                def stage1(tix, c0, n):
                    b = tix % 2
                    bk = 6 + b
                    fns = []
                    for j in range(4):
                        for k in range(KD):
                            fns.append(lambda e, j=j, k=k: e.matmul(
                                ps[bk][0:n, j * 128:(j + 1) * 128], lhsT=hT[:, k, c0:c0 + n], rhs=wring[s_vc[j]][:, k, :],
                                start=(k == 0), stop=(k == KD - 1)))
                    P.pe(fns, reads=hres + [('w', s) for s in s_vc], writes=[('ps', bk)])
                    P.act(lambda e: e.activation(out=vg[b][0:n, :], in_=ps[bk][0:n, :], func=GELU_FUNC),
                          reads=[('ps', bk)], writes=[('vg', b)])
                    P.dve(lambda e: e.bn_stats(out=st6[0:n, :], in_=vg[b][0:n, :]), reads=[('vg', b)], writes=[('st6',)])
                    P.dve(lambda e: e.bn_aggr(out=mv[0:n, :], in_=st6[0:n, :]), reads=[('st6',)], writes=[('mv',)])
                    P.act(lambda e: e.activation(out=mv[0:n, 1:2], in_=mv[0:n, 1:2], func=AF.Sqrt, bias=epsT[0:n, 0:1], scale=1.0),
                          reads=[('mv',), ('epsT',)], writes=[('mv',)])
                    P.dve(lambda e: e.reciprocal(out=mv[0:n, 1:2], in_=mv[0:n, 1:2]),
                          reads=[('mv',)], writes=[('mv',)])
                    P.dve(lambda e: e.tensor_scalar(out=vg[b][0:n, :], in0=vg[b][0:n, :], scalar1=mv[0:n, 0:1],
                                                    scalar2=mv[0:n, 1:2], op0=ALU.subtract, op1=ALU.mult),
                          reads=[('vg', b), ('mv',)], writes=[('vg', b)])
                    P.dve(lambda e: e.tensor_tensor(out=vg[b][0:n, :], in0=vg[b][0:n, :], in1=gbc[0:n, :], op=ALU.mult),
                          reads=[('vg', b), ('gbc',)], writes=[('vg', b)])
                    P.dve(lambda e: e.tensor_tensor(out=vg[b][0:n, :], in0=vg[b][0:n, :], in1=bbc[0:n, :], op=ALU.add),
                          reads=[('vg', b), ('bbc',)], writes=[('vg', b)])
                    P.act(lambda e: e.activation(out=vnb[b][0:n, :], in_=vg[b][0:n, :], func=AF.Copy),
                          reads=[('vg', b)], writes=[('vnb', b)])
                    if n == NS:
                        P.dma('sp', [lambda e: e.dma_start(out=vs[l], in_=vg[b][0:NS, :])],
                              reads=[('vg', b)], writes=[('o_vs', l)])

                def stage2(tix, c0, n):
                    b = tix % 2
                    for h in range(4):
                        bk2 = SA[h % 3] if (h + tix) % 2 == 0 else SB[h % 3]
                        if n == 128:
                            P.pe([lambda e: e.matmul(ps[bk2][:, 0:128], lhsT=vnb[b][:, h * 128:(h + 1) * 128],
                                                     rhs=Wt[:, h, :], start=True, stop=True)],
                                 reads=[('vnb', b), ('Wt',)], writes=[('ps', bk2)])
                            P.dve(lambda e: e.tensor_tensor(out=tmpc[:], in0=ps[bk2][:, 0:128], in1=bsb[:, h, :], op=ALU.add),
                                  reads=[('ps', bk2), ('bsb',)], writes=[('tmpc',)])
                            P.dve(lambda e: e.tensor_tensor(out=mixT[:, 12 + h, c0:c0 + 128], in0=tmpc[:],
                                                            in1=ucg[:, h, c0:c0 + 128], op=ALU.mult),
                                  reads=[('tmpc',), ('ucg', h)], writes=[('mixT', 12 + h)])
                        else:
                            P.pe([lambda e: e.matmul(ps[bk2][:, 0:NS], lhsT=vnb[b][0:NS, h * 128:(h + 1) * 128],
                                                     rhs=rhs_s[:, h, :], start=True, stop=True)],
                                 reads=[('vnb', b), ('rhs_s',)], writes=[('ps', bk2)])
                            P.dve(lambda e: e.scalar_tensor_tensor(
                                out=mixT[:, 12 + h, NP:TT], in0=ps[bk2][:, 0:NS], scalar=b0T[:, h:h + 1],
                                in1=ucg[:, h, NP:TT], op0=ALU.add, op1=ALU.mult),
                                reads=[('ps', bk2), ('b0T',), ('ucg', h)], writes=[('mixT', 12 + h)])

                stage1(0, *ttl[0])
                for tix, (c0, n) in enumerate(ttl):
                    if tix + 1 < len(ttl):
                        stage1(tix + 1, *ttl[tix + 1])
                    stage2(tix, c0, n)
            P.barrier()

        def out_proj(sb, l, tl, TS, has_s):
            for dblk in range(KD):
                s = load_w(wview(w_out[l], 0, dblk * 128))
                bs1 = next_bset()
                mm_w(s, mixT, mres, bs1, tl)
                for ti, (c0, n) in enumerate(tl):
                    P.dve(lambda e: e.tensor_tensor(out=xT[:, dblk, c0:c0 + n], in0=ps[bs1[ti]][:, 0:n],
                                                    in1=xT[:, dblk, c0:c0 + n], op=ALU.add),
                          reads=[('ps', bs1[ti]), ('xT', dblk)], writes=[('xT', dblk)])
            P.barrier()
            if has_s:
                with ExitStack() as ph:
                    ost = tmp(ph, "ost", [128, 3, 128], F32)
                    P.pe([tr(xasF[:].rearrange("p j b -> p (j b)"), 128, ps[6][:, 0:128])],
                         reads=[('xasF',), ('ident',)], writes=[('ps', 6)])
                    P.act(lambda e: e.activation(out=ost[:, 0, :], in_=ps[6][:, 0:128], func=AF.Copy),
                          reads=[('ps', 6)], writes=[('ost', 0)])
                    P.dma('sp', [lambda e, j=j: e.dma_start(out=cas[l][:, 2, j * 128:(j + 1) * 128],
                                                            in_=ost[j * NS:(j + 1) * NS, 0, :]) for j in range(8)],
                          reads=[('ost', 0)], writes=[('o_cas', l)])
                    P.pe([tr(hsF[:].rearrange("p j b -> p (j b)"), 128, ps[7][:, 0:128])],
                         reads=[('hsF',), ('ident',)], writes=[('ps', 7)])
                    P.act(lambda e: e.activation(out=ost[:, 1, :], in_=ps[7][:, 0:128], func=AF.Copy),
                          reads=[('ps', 7)], writes=[('ost', 1)])
                    P.dma('sp', [lambda e, j=j: e.dma_start(out=hs[l][:, j * 128:(j + 1) * 128],
                                                            in_=ost[j * NS:(j + 1) * NS, 1, :]) for j in range(8)],
                          reads=[('ost', 1)], writes=[('o_hs', l)])
                    P.pe([tr(ubsF[:].rearrange("p j b -> p (j b)"), 128, ps[6][0:64, 0:128])],
                         reads=[('ubsF',), ('ident',)], writes=[('ps', 6)])
                    P.act(lambda e: e.activation(out=ost[0:64, 2, :], in_=ps[6][0:64, 0:128], func=AF.Copy),
                          reads=[('ps', 6)], writes=[('ost', 2)])
                    P.dma('sp', [lambda e, j=j: e.dma_start(out=cbs[l][:, 29, j * 128:(j + 1) * 128],
                                                            in_=ost[j * NS:(j + 1) * NS, 2, :]) for j in range(4)],
                          reads=[('ost', 2)], writes=[('o_cbs', l)])
                P.barrier()

        def ffn(sb, l, tl, TS, has_s):
            with ExitStack() as ph:
                rl = [tmp(ph, "rl", [128, TT], F32) for i in range(2)]
                ffT = mixT
                for g in range(4):
                    for fb in range(16):
                        s = load_w(wview(w_ff1[l], 0, (g * 16 + fb) * 128))
                        bs1 = next_bset()
                        mm_w(s, hT, hres, bs1, tl)
                        b = fb % 2
                        for ti, (c0, n) in enumerate(tl):
                            P.act(lambda e: e.activation(out=rl[b][:, c0:c0 + n], in_=ps[bs1[ti]][:, 0:n], func=AF.Relu),
                                  reads=[('ps', bs1[ti])], writes=[('rl', b)])
                        P.dve(lambda e: e.tensor_tensor(out=ffT[:, fb, 0:TS], in0=rl[b][:, 0:TS], in1=rl[b][:, 0:TS],
                                                        op=ALU.mult),
                              reads=[('rl', b)], writes=[('mixT', fb)])
                    for dblk in range(KD):
                        s = load_w(wview(w_ff2[l], g * 2048, dblk * 128))
                        bs1 = next_bset()
                        mm_w(s, ffT, mres, bs1, tl)
                        for ti, (c0, n) in enumerate(tl):
                            P.dve(lambda e: e.tensor_tensor(out=xT[:, dblk, c0:c0 + n], in0=ps[bs1[ti]][:, 0:n],
                                                            in1=xT[:, dblk, c0:c0 + n], op=ALU.add),
                                  reads=[('ps', bs1[ti]), ('xT', dblk)], writes=[('xT', dblk)])
            P.barrier()

        def load_tokens(sb, has_s):
            with ExitStack() as ph:
                stg = [tmp(ph, "stg", [128, D], F32) for i in range(2)]

                def load_T(rows_ap, n, col0, bi):
                    P.dma('sp', [lambda e: e.dma_start(out=stg[bi][0:n, :], in_=rows_ap)], writes=[('stg', bi)])
                    for q in range(4):
                        bk = 6 + (q % 2)
                        fns = [tr(stg[bi][0:n, (4 * q + i) * 128:(4 * q + i + 1) * 128], n,
                                  ps[bk][:, i * n:(i + 1) * n]) for i in range(4)]
                        P.pe(fns, reads=[('stg', bi), ('ident',)], writes=[('ps', bk)])
                        P.act(lambda e: e.activation(out=xT[:, 4 * q:4 * q + 4, col0:col0 + n],
                                                     in_=ps[bk][:, 0:4 * n].rearrange("p (i n) -> p i n", n=n), func=AF.Copy),
                              reads=[('ps', bk)], writes=xres(range(4 * q, 4 * q + 4)))
                for tt in range(8):
                    r0 = sb * NP + tt * 128
                    load_T(xp[r0:r0 + 128, :], 128, tt * 128, tt % 2)
                if has_s:
                    load_T(xs[:, :], NS, NP, 0)
            P.barrier()

        def store_tokens(sb, has_s):
            with ExitStack() as ph:
                ost = [tmp(ph, "yst", [128, D], F32) for i in range(2)]

                def store_T(rows_ap, n, col0, bi):
                    for q in range(4):
                        bk = 6 + (q % 2)
                        fns = [tr(xT[:, 4 * q + i, col0:col0 + n], 128, ps[bk][0:n, i * 128:(i + 1) * 128]) for i in range(4)]
                        P.pe(fns, reads=xres(range(4 * q, 4 * q + 4)) + [('ident',)], writes=[('ps', bk)])
                        P.act(lambda e: e.activation(out=ost[bi][0:n, q * 512:(q + 1) * 512], in_=ps[bk][0:n, :], func=AF.Copy),
                              reads=[('ps', bk)], writes=[('yst', bi)])
                    P.dma('sp', [lambda e: e.dma_start(out=rows_ap, in_=ost[bi][0:n, :])], reads=[('yst', bi)],
                          writes=[('o_y', col0, sb)])
                for tt in range(8):
                    r0 = sb * NP + tt * 128
                    store_T(yp[r0:r0 + 128, :], 128, tt * 128, tt % 2)
                if has_s:
                    store_T(ys[:, :], NS, NP, 0)
            P.barrier()

        kstop = int(_os.environ.get('KSTOP', '0'))
        pc = [0]

        def chk():
            pc[0] += 1
            if kstop and pc[0] >= kstop:
                _DEAD[0] = True
        try:
            for sb in range(2):
                has_s = (sb == 0)
                tl = [(0, 512), (512, 512)] + ([(NP, NS)] if has_s else [])
                TS = TT if has_s else NP
                load_tokens(sb, has_s)
                chk()
                for l in range(2):
                    do_consts(sb, l, has_s)
                    chk()
                    rmsnorm(lambda k: cT[:, 40 + k:41 + k], True, tl, TS)
                    chk()
                    mixer_a(sb, l, tl, TS, has_s)
                    chk()
                    mixer_b(sb, l, tl, TS, has_s)
                    chk()
                    mixer_c(sb, l, tl, TS, has_s)
                    chk()
                    out_proj(sb, l, tl, TS, has_s)
                    chk()
                    rmsnorm(lambda k: cT[:, 56 + k:57 + k], True, tl, TS)
                    ffn(sb, l, tl, TS, has_s)
                    chk()
                rmsnorm(lambda k: gfT[:, k:k + 1], False, tl, TS)
                store_tokens(sb, has_s)
                chk()
        except _Stop:
            pass

        with ExitStack() as ph:
            ost = tmp(ph, "tst", [128, 3, 128], F32)
            hpad = tmp(ph, "hpad", [128, 32], F32)
            for l in range(2):
                P.pe([tr(caT[:, l, :, :].rearrange("p j k -> p (j k)"), 128, ps[6][0:24, 0:128])],
                     reads=[('caT',), ('ident',)], writes=[('ps', 6)])
                P.act(lambda e: e.activation(out=ost[0:24, 0, :], in_=ps[6][0:24, 0:128], func=AF.Copy),
                      reads=[('ps', 6)], writes=[('tst', 0)])
                P.dma('sp', [lambda e, j=j: e.dma_start(out=cap[l][:, j * 128:(j + 1) * 128], in_=ost[j * 3:(j + 1) * 3, 0, :])
                             for j in range(8)],
                      reads=[('tst', 0)], writes=[('o_cap', l)])
                sub(60 + 3 * l)
                P.pe([tr(cbT[:, l, :, :].rearrange("p j k -> p (j k)"), 128, ps[7][0:120, 0:128])],
                     reads=[('cbT',), ('ident',)], writes=[('ps', 7)])
                P.act(lambda e: e.activation(out=ost[0:120, 1, :], in_=ps[7][0:120, 0:128], func=AF.Copy),
                      reads=[('ps', 7)], writes=[('tst', 1)])
                P.dma('sp', [lambda e, j=j: e.dma_start(out=cbp[l][:, j * 128:(j + 1) * 128], in_=ost[j * 30:(j + 1) * 30, 1, :])
                             for j in range(4)],
                      reads=[('tst', 1)], writes=[('o_cbp', l)])
                sub(61 + 3 * l)
                P.dve(lambda e: e.memset(hpad[:], 0.0), writes=[('hpad',)])
                P.dve(lambda e: e.tensor_copy(out=hpad[:, 0:8], in_=hcT[:, l, :]), reads=[('hcT',)], writes=[('hpad',)])
                P.pe([tr(hpad[:], 128, ps[6][0:32, 0:128])], reads=[('hpad',), ('ident',)], writes=[('ps', 6)])
                P.act(lambda e: e.activation(out=ost[0:8, 2, :], in_=ps[6][0:8, 0:128], func=AF.Copy),
                      reads=[('ps', 6)], writes=[('tst', 2)])
                P.dma('sp', [lambda e: e.dma_start(out=hp[l].rearrange("(j c) -> j c", c=128), in_=ost[0:8, 2, :])],
                      reads=[('tst', 2)], writes=[('o_hp', l)])
            P.final_wait('sp')

            with nc.Block() as block:
                @block.tensor
                def _(e):
                    _replay(e, P.streams['pe'])

                @block.scalar
                def _(e):
                    _replay(e, P.streams['act'])

                @block.vector
                def _(e):
                    _replay(e, P.streams['dve'])

                @block.gpsimd
                def _(e):
                    _replay(e, P.streams['pool'])

                @block.sync
                def _(e):
                    _replay(e, P.streams['sp'])
    return nc


_NC_CACHE = {}

WEIGHT_KEYS = ['norm_mix', 'w_in', 'conv_a_w', 'conv_a_b', 'gate_r_w', 'gate_r_b', 'gate_i_w', 'gate_i_b',
               'lru_lambda', 'conv_b_w', 'ln_b_g', 'ln_b_b', 'sgu_ln_g', 'sgu_ln_b', 'sgu_w', 'sgu_b',
               'w_out', 'norm_ffn', 'w_ff1', 'w_ff2', 'norm_final']


def kernel(**inputs):
    f = lambda a: np.ascontiguousarray(np.asarray(a, dtype=np.float32))
    x_prompt = f(inputs['x_prompt'])
    x_sample = f(inputs['x_sample'])
    sca = f(inputs['state_conv_a'])
    slh = f(inputs['state_lru_h'])
    scb = f(inputs['state_conv_b'])
    W = {k: f(inputs[k]) for k in WEIGHT_KEYS}
    if 'nc' not in _NC_CACHE:
        _NC_CACHE['nc'] = build_nc()
    nc = _NC_CACHE['nc']
    in_maps = []
    for c in range(8):
        b0 = c * NS
        m = {'xp': x_prompt[c % 4], 'xs': f(x_sample[b0:b0 + NS, 0, :]),
             'sca': f(sca[:, b0:b0 + NS]), 'slh': f(slh[:, b0:b0 + NS]), 'scb': f(scb[:, b0:b0 + NS])}
        m.update(W)
        in_maps.append(m)
    res = run_bass_kernel_spmd(nc, in_maps, core_ids=list(range(8)))
    R = res.results
    y_prompt = np.stack([np.asarray(R[c]['yp']) for c in range(4)], axis=0)
    y_sample = np.concatenate([np.asarray(R[c]['ys']) for c in range(8)], axis=0)[:, None, :]
    ca_p = np.stack([np.asarray(R[c]['cap']) for c in range(4)], axis=1)
    h_p = np.stack([np.asarray(R[c]['hp']) for c in range(4)], axis=1)
    cb_p = np.stack([np.asarray(R[c]['cbp']) for c in range(4)], axis=1)
    ca_s = np.concatenate([np.asarray(R[c]['cas']) for c in range(8)], axis=1)
    h_s = np.concatenate([np.asarray(R[c]['hs']) for c in range(8)], axis=1)
    cb_s = np.concatenate([np.asarray(R[c]['cbs']) for c in range(8)], axis=1)
    v_s = np.concatenate([np.asarray(R[c]['vs']) for c in range(8)], axis=1)[:, :, None, :]
    outs = (y_prompt, y_sample, ca_p, h_p, cb_p, ca_s, h_s, cb_s, v_s)
    return tuple(np.ascontiguousarray(o.astype(np.float32)) for o in outs)
```

```python
import numpy as np
from contextlib import ExitStack
import concourse.bass as bass
import concourse.mybir as mybir
from concourse.bass_utils import run_bass_kernel_spmd

_DEAD = [False]
F32 = mybir.dt.float32
BF16 = mybir.dt.bfloat16
AF = mybir.ActivationFunctionType
ALU = mybir.AluOpType

D = 2048
KD = 16
NP = 1024
NS = 16
TT = NP + NS
NW = 6
EPS = 1e-6
GELU_FUNC = AF.Gelu_apprx_tanh


class Rec:
    def __init__(self):
        self.calls = []

    def __getattr__(self, name):
        def f(*a, **kw):
            self.calls.append((name, a, kw))
            return self
        return f


def _record(fns):
    calls = []
    for f in fns:
        r = Rec()
        f(r)
        calls.extend(r.calls)
    return calls


def _replay(e, ops):
    for op in ops:
        if op[0] == 'wait':
            e.wait_ge(op[1], op[2])
        else:
            _, name, a, kw, sem, inc = op
            ins = getattr(e, name)(*a, **kw)
            if sem is not None:
                ins.then_inc(sem, inc)


class Chan:
    def __init__(self, sem):
        self.sem = sem
        self.val = 0


class Prog:
    def __init__(self):
        self.streams = {e: [] for e in ('pe', 'act', 'dve', 'pool', 'sp')}
        self.chan = {}
        self.known = {e: {} for e in self.streams}
        self.clock = {}
        self.lastw = {}
        self.readers = {}
        self.wslot = 0
        self.spc = 0

    def add_chan(self, name, sem):
        self.chan[name] = Chan(sem)

    def _deps(self, reads, writes):
        deps = {}

        def need(tok):
            if tok is None:
                return
            c, v = tok
            if deps.get(c, 0) < v:
                deps[c] = v
        for r in reads:
            need(self.lastw.get(r))
        for w in writes:
            need(self.lastw.get(w))
            for c, v in self.readers.get(w, {}).items():
                need((c, v))
        return deps

    def _emit_waits(self, eng, deps):
        kn = self.known[eng]
        ops = self.streams[eng]
        for c, v in deps.items():
            if eng == 'pe' and c == 'pe':
                continue
            if kn.get(c, 0) >= v:
                continue
            sem = self.chan[c].sem
            ops.append(('wait', sem, v))
            for c2, v2 in self.clock[(c, v)].items():
                if kn.get(c2, 0) < v2:
                    kn[c2] = v2

    def _record(self, tok, reads, writes):
        c, v = tok
        for r in reads:
            d = self.readers.setdefault(r, {})
            if d.get(c, 0) < v:
                d[c] = v
        for w in writes:
            self.lastw[w] = tok
            self.readers[w] = {}

    def unit(self, eng, fns, reads=(), writes=()):
        if _DEAD[0]:
            return
        if not isinstance(fns, (list, tuple)):
            fns = [fns]
        writes = list(writes) + [r for r in reads if r[0] == 'ps']
        reads = [r for r in reads if r[0] != 'ps']
        deps = self._deps(reads, writes)
        self._emit_waits(eng, deps)
        ch = self.chan[eng]
        ch.val += 1
        tok = (eng, ch.val)
        clk = dict(self.known[eng])
        clk[eng] = ch.val
        self.clock[tok] = clk
        ops = self.streams[eng]
        calls = _record(fns)
        for i, (name, a, kw) in enumerate(calls):
            lastc = (i == len(calls) - 1)
            ops.append(('call', name, a, kw, ch.sem if lastc else None, 1))
        self._record(tok, reads, writes)

    def dma(self, eng, fns, reads=(), writes=(), chan=None):
        if _DEAD[0]:
            return
        if not isinstance(fns, (list, tuple)):
            fns = [fns]
        if chan is None:
            chan = 'sp%d' % self.spc
            self.spc = (self.spc + 1) % 8
        deps = self._deps(reads, writes)
        ch = self.chan[chan]
        if ch.val > 0 and deps.get(chan, 0) < ch.val:
            deps[chan] = ch.val
        self._emit_waits(eng, deps)
        calls = _record(fns)
        ch.val += 16 * len(calls)
        tok = (chan, ch.val)
        clk = dict(self.known[eng])
        clk[chan] = ch.val
        self.clock[tok] = clk
        ops = self.streams[eng]
        for (name, a, kw) in calls:
            ops.append(('call', name, a, kw, ch.sem, 16))
        self._record(tok, reads, writes)

    def pe(self, fns, reads=(), writes=()):
        self.unit('pe', fns, reads, writes)

    def act(self, fns, reads=(), writes=()):
        self.unit('act', fns, reads, writes)

    def dve(self, fns, reads=(), writes=()):
        self.unit('dve', fns, reads, writes)

    def pool(self, fns, reads=(), writes=()):
        self.unit('pool', fns, reads, writes)

    def barrier(self, engines=('pe', 'act', 'dve', 'sp')):
        if _DEAD[0]:
            return
        for e in engines:
            deps = {}
            for c, ch in self.chan.items():
                if c.startswith('w'):
                    continue
                if ch.val > 0:
                    deps[c] = ch.val
            self._emit_waits(e, deps)
        for k in list(self.lastw.keys()):
            if k[0] != 'w':
                del self.lastw[k]
        for k in list(self.readers.keys()):
            if k[0] != 'w':
                del self.readers[k]

    def final_wait(self, eng='sp'):
        deps = {c: ch.val for c, ch in self.chan.items() if ch.val > 0}
        self._emit_waits(eng, deps)


class _Stop(Exception):
    pass


def sub(n):
    return


def build_nc():
    nc = bass.Bass("TRN2", target_bir_lowering=False)

    def din(name, shape):
        return nc.dram_tensor(name, list(shape), F32, kind="ExternalInput").ap()

    def dout(name, shape):
        return nc.dram_tensor(name, list(shape), F32, kind="ExternalOutput").ap()

    xp = din("xp", [2048, D])
    xs = din("xs", [NS, D])
    sca = din("sca", [2, NS, 3, 1024])
    slh = din("slh", [2, NS, 1024])
    scb = din("scb", [2, NS, 30, 512])
    norm_mix = din("norm_mix", [2, D])
    w_in = din("w_in", [2, D, 4096])
    conv_a_w = din("conv_a_w", [2, 4, 1024])
    conv_a_b = din("conv_a_b", [2, 1024])
    gate_r_w = din("gate_r_w", [2, 8, 128, 128])
    gate_r_b = din("gate_r_b", [2, 1024])
    gate_i_w = din("gate_i_w", [2, 8, 128, 128])
    gate_i_b = din("gate_i_b", [2, 1024])
    lru_lambda = din("lru_lambda", [2, 1024])
    conv_b_w = din("conv_b_w", [2, 31, 512])
    ln_b_g = din("ln_b_g", [2, 512])
    ln_b_b = din("ln_b_b", [2, 512])
    sgu_ln_g = din("sgu_ln_g", [2, 512])
    sgu_ln_b = din("sgu_ln_b", [2, 512])
    sgu_w = din("sgu_w", [2, 4, 128, 128])
    sgu_b = din("sgu_b", [2, 4, 128])
    w_out = din("w_out", [2, D, D])
    norm_ffn = din("norm_ffn", [2, D])
    w_ff1 = din("w_ff1", [2, D, 8192])
    w_ff2 = din("w_ff2", [2, 8192, D])
    norm_final = din("norm_final", [D])

    yp = dout("yp", [2048, D])
    ys = dout("ys", [NS, D])
    cap = dout("cap", [2, 3, 1024])
    hp = dout("hp", [2, 1024])
    cbp = dout("cbp", [2, 30, 512])
    cas = dout("cas", [2, NS, 3, 1024])
    hs = dout("hs", [2, NS, 1024])
    cbs = dout("cbs", [2, NS, 30, 512])
    vs = dout("vs", [2, NS, 512])

    P = Prog()
    uid = [0]
    es = ExitStack()
    with es:
        def sb_(name, shape, dt=F32):
            return es.enter_context(nc.sbuf_tensor(name, list(shape), dt))

        def tmp(ph, name, shape, dt=F32):
            uid[0] += 1
            return ph.enter_context(nc.sbuf_tensor("%s_u%d" % (name, uid[0]), list(shape), dt))

        def sem_(name):
            return es.enter_context(nc.semaphore(name))

        for e in ('pe', 'act', 'dve', 'pool'):
            P.add_chan(e, sem_("s_" + e))
        for i in range(8):
            P.add_chan('sp%d' % i, sem_("s_sp%d" % i))
        for i in range(NW):
            P.add_chan('w%d' % i, sem_("s_w%d" % i))
        for i in range(2):
            P.add_chan('wg%d' % i, sem_("s_wg%d" % i))

        with nc.sbuf_tensor("zinit", [128, 3, 17600], F32) as zz:
            P.dve(lambda e: e.memset(zz[:, 0, :], 0.0), writes=[('zz', 0)])
            P.dve(lambda e: e.memset(zz[:, 1, :], 0.0), writes=[('zz', 1)])
            P.pool(lambda e: e.memset(zz[:, 2, :], 0.0), writes=[('zz', 2)])
        P.barrier(engines=('pe', 'act', 'dve', 'sp', 'pool'))
        xT = sb_("xT", [128, KD, TT], F32)
        hT = sb_("hT", [128, KD, TT], BF16)
        mixT = sb_("mixT", [128, KD, TT], BF16)
        wring = [sb_("wr%d" % i, [128, KD, 128], BF16) for i in range(NW)]
        wg = [sb_("wg%d" % i, [128, 2, 128], BF16) for i in range(2)]
        ident = sb_("ident", [128, 128], F32)
        identb = sb_("identb", [128, 128], BF16)
        onesb = sb_("onesb", [128, 128], BF16)
        onesf = sb_("onesf", [128, 128], F32)
        tri = sb_("tri", [128, 128], F32)
        epsT = sb_("epsT", [128, 1], F32)
        oneT = sb_("oneT", [128, 1], F32)
        cT = sb_("cT", [128, 104], F32)
        cbw = sb_("cbw", [128, 124], F32)
        gfT = sb_("gfT", [128, 16], F32)
        c1 = sb_("c1", [128, 8], F32)
        c2 = sb_("c2", [128, 8], F32)
        caT = sb_("caT", [128, 2, 8, 3], F32)
        cbT = sb_("cbT", [128, 2, 4, 30], F32)
        hcT = sb_("hcT", [128, 2, 8], F32)
        SaT = sb_("SaT", [128, 8, 48], BF16)
        SbT = sb_("SbT", [128, 4, 480], BF16)
        h0T = sb_("h0T", [128, 8, 16], F32)
        xasF = sb_("xasF", [128, 8, 16], F32)
        ubsF = sb_("ubsF", [128, 4, 16], F32)
        hsF = sb_("hsF", [128, 8, 16], F32)
        scr = sb_("scr", [128, 8], F32)

        ps = [es.enter_context(nc.psum_tensor("ps%d" % i, [128, 512], F32)) for i in range(8)]
        SA = [0, 1, 2]
        SB = [3, 4, 5]

        def xres(ks=range(KD)):
            return [('xT', k) for k in ks]

        hres = [('hT', k) for k in range(KD)]
        mres = [('mixT', k) for k in range(KD)]

        def load_w(src_ap):
            s = P.wslot
            P.wslot = (s + 1) % NW
            P.dma('pool', [lambda e: e.dma_start(out=wring[s][:], in_=src_ap)],
                  writes=[('w', s)], chan='w%d' % s)
            return s

        def wview(wfull, r0, c0):
            return wfull[r0:r0 + 2048, c0:c0 + 128].rearrange("(k p) c -> p k c", p=128)

        bset_state = [0]

        def next_bset():
            bset_state[0] ^= 1
            return SA if bset_state[0] else SB

        def mm_w(slot, src, src_res, bset, tl):
            wt = wring[slot]
            fns = []
            for k in range(KD):
                for ti, (c0, n) in enumerate(tl):
                    fns.append(lambda e, ti=ti, c0=c0, n=n, k=k: e.matmul(
                        ps[bset[ti]][:, 0:n], lhsT=wt[:, k, :], rhs=src[:, k, c0:c0 + n],
                        start=(k == 0), stop=(k == KD - 1)))
            P.pe(fns, reads=[('w', slot)] + list(src_res), writes=[('ps', bset[ti]) for ti in range(len(tl))])

        def tr(in_ap, n_in_part, out_ap):
            return lambda e: e.transpose(out=out_ap, in_=in_ap, identity=ident[0:n_in_part, 0:n_in_part])

        P.pool(lambda e: e.memset(onesf[:], 1.0), writes=[('onesf',)])
        P.pool(lambda e: e.memset(epsT[:], EPS), writes=[('epsT',)])
        P.pool(lambda e: e.memset(oneT[:], 1.0), writes=[('oneT',)])
        P.pool(lambda e: e.affine_select(out=ident[:], in_=onesf[:], pattern=[[1, 128]], compare_op=ALU.is_equal,
                                         fill=0.0, base=0, channel_multiplier=-1),
               reads=[('onesf',)], writes=[('ident',)])
        P.pool(lambda e: e.affine_select(out=tri[:], in_=onesf[:], pattern=[[1, 128]], compare_op=ALU.is_ge,
                                         fill=0.0, base=0, channel_multiplier=-1),
               reads=[('onesf',)], writes=[('tri',)])
        P.dve(lambda e: e.tensor_copy(out=identb[:], in_=ident[:]), reads=[('ident',)], writes=[('identb',)])
        P.dve(lambda e: e.tensor_copy(out=onesb[:], in_=onesf[:]), reads=[('onesf',)], writes=[('onesb',)])
        P.dve(lambda e: e.memset(caT[:], 0.0), writes=[('caT',)])
        P.dve(lambda e: e.memset(cbT[:], 0.0), writes=[('cbT',)])
        P.dve(lambda e: e.memset(hcT[:], 0.0), writes=[('hcT',)])
        with ExitStack() as ph:
            gst = tmp(ph, "gst", [16, 128], F32)
            P.dma('sp', [lambda e: e.dma_start(out=gst[:], in_=norm_final.rearrange("(n c) -> n c", c=128))],
                  writes=[('gst',)])
            P.pe([tr(gst[0:16, :], 16, ps[6][:, 0:16])], reads=[('gst',), ('ident',)], writes=[('ps', 6)])
            P.dve(lambda e: e.tensor_copy(out=gfT[:], in_=ps[6][:, 0:16]), reads=[('ps', 6)], writes=[('gfT',)])
        P.barrier()

        def rmsnorm(gcol, to_hT, tl, TS):
            with ExitStack() as ph:
                sq = [tmp(ph, "sq", [128, TT], BF16) for i in range(2)]
                rstd = tmp(ph, "rstd", [128, TT], F32)
                for k in range(KD):
                    b = k % 2
                    P.act(lambda e: e.activation(out=sq[b][:, 0:TS], in_=xT[:, k, 0:TS], func=AF.Square),
                          reads=[('xT', k)], writes=[('sq', b)])
                    fns = [lambda e, ti=ti, c0=c0, n=n: e.matmul(
                        ps[SA[ti]][:, 0:n], lhsT=onesb[:], rhs=sq[b][:, c0:c0 + n],
                        start=(k == 0), stop=(k == KD - 1)) for ti, (c0, n) in enumerate(tl)]
                    P.pe(fns, reads=[('sq', b), ('onesb',)], writes=[('ps', SA[ti]) for ti in range(len(tl))])
                for ti, (c0, n) in enumerate(tl):
                    P.act(lambda e: e.activation(out=rstd[:, c0:c0 + n], in_=ps[SA[ti]][:, 0:n], func=AF.Sqrt,
                                                 bias=epsT[:, 0:1], scale=1.0 / D),
                          reads=[('ps', SA[ti]), ('epsT',)], writes=[('rstd', ti)])
                    P.dve(lambda e: e.reciprocal(out=rstd[:, c0:c0 + n], in_=rstd[:, c0:c0 + n]),
                          reads=[('rstd', ti)], writes=[('rstd', ti)])
                rr = [('rstd', ti) for ti in range(len(tl))]
                for k in range(KD):
                    dst = hT if to_hT else xT
                    P.dve(lambda e: e.scalar_tensor_tensor(
                        out=dst[:, k, 0:TS], in0=xT[:, k, 0:TS], scalar=gcol(k), in1=rstd[:, 0:TS],
                        op0=ALU.mult, op1=ALU.mult),
                        reads=[('xT', k), ('cT',), ('gfT',)] + rr, writes=[('hT', k) if to_hT else ('xT', k)])
            P.barrier()

        def do_consts(sb, l, has_s):
            with ExitStack() as ph:
                cst = tmp(ph, "cst", [128, 128], F32)
                cst2 = tmp(ph, "cst2", [128, 128], F32)
                vecs = [(conv_a_b[l], 8), (gate_r_b[l], 8), (gate_i_b[l], 8), (lru_lambda[l], 8),
                        (ln_b_g[l], 4), (ln_b_b[l], 4), (norm_mix[l], 16), (norm_ffn[l], 16)]
                fns = []
                r = 0
                for v, n in vecs:
                    fns.append(lambda e, v=v, n=n, r=r: e.dma_start(
                        out=cst[r:r + n, :], in_=v.rearrange("(n c) -> n c", c=128)))
                    r += n
                fns.append(lambda e: e.dma_start(out=cst[72:104, :],
                                                 in_=conv_a_w[l].rearrange("k (j c) -> (k j) c", c=128)))
                fns.append(lambda e: e.dma_start(out=cst2[0:124, :],
                                                 in_=conv_b_w[l].rearrange("k (j c) -> (k j) c", c=128)))
                P.dma('sp', fns, writes=[('cst',), ('cst2',)])
                P.pe([tr(cst[0:104, :], 104, ps[6][:, 0:104])], reads=[('cst',), ('ident',)], writes=[('ps', 6)])
                P.dve(lambda e: e.tensor_copy(out=cT[:], in_=ps[6][:, 0:104]), reads=[('ps', 6)], writes=[('cT',)])
                P.pe([tr(cst2[0:124, :], 124, ps[7][:, 0:124])], reads=[('cst2',), ('ident',)], writes=[('ps', 7)])
                P.dve(lambda e: e.tensor_copy(out=cbw[:], in_=ps[7][:, 0:124]), reads=[('ps', 7)], writes=[('cbw',)])
                P.act(lambda e: e.activation(out=scr[:], in_=cT[:, 24:32], func=AF.Exp, scale=-1.0),
                      reads=[('cT',)], writes=[('scr',)])
                P.act(lambda e: e.activation(out=scr[:], in_=scr[:], func=AF.Ln, bias=oneT[:, 0:1], scale=1.0),
                      reads=[('scr',), ('oneT',)], writes=[('scr',)])
                P.dve(lambda e: e.tensor_scalar_mul(out=c1[:], in0=scr[:], scalar1=-8.0),
                      reads=[('scr',)], writes=[('c1',)])
                P.dve(lambda e: e.tensor_scalar_mul(out=c2[:], in0=scr[:], scalar1=-16.0),
                      reads=[('scr',)], writes=[('c2',)])
                if has_s:
                    sst = tmp(ph, "sst", [128, 4, 512], F32)
                    P.dma('sp', [lambda e: e.dma_start(out=sst[0:48, 0:2, :].rearrange("p a c -> p (a c)"),
                                                       in_=sca[l].rearrange("b k c -> (b k) c"))],
                          writes=[('sst',)])
                    for q in range(2):
                        bk = 6 + q
                        fns = [tr(sst[0:48, (4 * q + i) // 4, ((4 * q + i) % 4) * 128:((4 * q + i) % 4 + 1) * 128],
                                  48, ps[bk][:, i * 48:(i + 1) * 48]) for i in range(4)]
                        P.pe(fns, reads=[('sst',), ('ident',)], writes=[('ps', bk)])
                        P.dve(lambda e: e.tensor_copy(out=SaT[:, 4 * q:4 * q + 4, :],
                                                      in_=ps[bk][:, 0:192].rearrange("p (i n) -> p i n", n=48)),
                              reads=[('ps', bk)], writes=[('SaT',)])
                    P.dma('sp', [lambda e: e.dma_start(out=sst[0:16, 0:2, :].rearrange("p a c -> p (a c)"), in_=slh[l])],
                          writes=[('sst',)])
                    fns = [tr(sst[0:16, j // 4, (j % 4) * 128:(j % 4 + 1) * 128], 16,
                              ps[6][:, j * 16:(j + 1) * 16]) for j in range(8)]
                    P.pe(fns, reads=[('sst',), ('ident',)], writes=[('ps', 6)])
                    P.dve(lambda e: e.tensor_copy(out=h0T[:], in_=ps[6][:, 0:128].rearrange("p (j n) -> p j n", n=16)),
                          reads=[('ps', 6)], writes=[('h0T',)])
                    P.dma('sp', [lambda e: e.dma_start(out=sst[0:120, :, :],
                                                       in_=scb[l].rearrange("(g b) k c -> (b k) g c", b=4))],
                          writes=[('sst',)])
                    for j in range(4):
                        bk = 6 + (j % 2)
                        fns = [tr(sst[0:120, g, j * 128:(j + 1) * 128], 120,
                                  ps[bk][:, g * 120:(g + 1) * 120]) for g in range(4)]
                        P.pe(fns, reads=[('sst',), ('ident',)], writes=[('ps', bk)])
                        P.dve(lambda e: e.tensor_copy(out=SbT[:, j, :], in_=ps[bk][:, 0:480]),
                              reads=[('ps', bk)], writes=[('SbT',)])
                    P.dma('sp', [lambda e: e.dma_start(out=cas[l][:, 0:2, :], in_=sca[l][:, 1:3, :]),
                                 lambda e: e.dma_start(out=cbs[l][:, 0:29, :], in_=scb[l][:, 1:30, :])],
                          writes=[('o_shift', l)])
            P.barrier()

        def mixer_a(sb, l, tl, TS, has_s):
            with ExitStack() as ph:
                xe = tmp(ph, "xa_ext", [128, 3 + NP], BF16)
                xasb = tmp(ph, "xasb", [128, NS], BF16)
                gg = tmp(ph, "gg", [128, TT], BF16)
                xc = tmp(ph, "xc", [128, TT], F32)
                xcb = tmp(ph, "xcb", [128, TT], BF16)
                rT = tmp(ph, "rT", [128, TT], F32)
                iT = tmp(ph, "iT", [128, TT], F32)
                aT = tmp(ph, "aT", [128, TT], F32)
                dgA = tmp(ph, "dgA", [128, 4, 128], BF16)
                hh = xc
                for j in range(8):
                    gb_ = j % 2
                    P.dma('pool', [lambda e: e.dma_start(out=wg[gb_][:, 0, :], in_=gate_r_w[l, j]),
                                   lambda e: e.dma_start(out=wg[gb_][:, 1, :], in_=gate_i_w[l, j])],
                          writes=[('w', 'g', gb_)], chan='wg%d' % gb_)
                    s_xa = load_w(wview(w_in[l], 0, j * 128))
                    s_ga = load_w(wview(w_in[l], 0, 1024 + j * 128))
                    bs1 = next_bset()
                    mm_w(s_xa, hT, hres, bs1, tl)
                    P.dve(lambda e: e.tensor_copy(out=xe[:, 0:3], in_=caT[:, l, j, :]),
                          reads=[('caT',)], writes=[('xa_ext',)])
                    for ti, (c0, n) in enumerate(tl[:2]):
                        P.act(lambda e: e.activation(out=xe[:, 3 + c0:3 + c0 + n], in_=ps[bs1[ti]][:, 0:n], func=AF.Copy),
                              reads=[('ps', bs1[ti])], writes=[('xa_ext',)])
                    P.act(lambda e: e.activation(out=caT[:, l, j, :], in_=ps[bs1[1]][:, 509:512], func=AF.Copy),
                          reads=[('ps', bs1[1])], writes=[('caT',)])
                    if has_s:
                        P.act(lambda e: e.activation(out=xasF[:, j, :], in_=ps[bs1[2]][:, 0:NS], func=AF.Copy),
                              reads=[('ps', bs1[2])], writes=[('xasF',)])
                        P.act(lambda e: e.activation(out=xasb[:], in_=ps[bs1[2]][:, 0:NS], func=AF.Copy),
                              reads=[('ps', bs1[2])], writes=[('xasb',)])
                    sub(1)
                    bs2 = next_bset()
                    mm_w(s_ga, hT, hres, bs2, tl)
                    for ti, (c0, n) in enumerate(tl):
                        P.act(lambda e: e.activation(out=gg[:, c0:c0 + n], in_=ps[bs2[ti]][:, 0:n], func=GELU_FUNC),
                              reads=[('ps', bs2[ti])], writes=[('gg',)])
                    sub(2)
                    for k in range(4):
                        P.dve(lambda e: e.tensor_scalar_mul(out=dgA[:, k, :], in0=identb[:],
                                                            scalar1=cT[:, 72 + k * 8 + j:73 + k * 8 + j]),
                              reads=[('identb',), ('cT',)], writes=[('dgA',)])
                    sub(21)
                    for ti, (c0, n) in enumerate(tl):
                        sub(22 + ti)
                        if ti < 2:
                            fns = [lambda e, k=k: e.matmul(ps[6][:, 0:n], lhsT=dgA[:, k, :], rhs=xe[:, c0 + k:c0 + k + n],
                                                           start=(k == 0), stop=(k == 3)) for k in range(4)]
                            P.pe(fns, reads=[('dgA',), ('xa_ext',)], writes=[('ps', 6)])
                        else:
                            fns = [lambda e: e.matmul(ps[6][:, 0:NS], lhsT=dgA[:, 3, :], rhs=xasb[:], start=True, stop=False)]
                            for k in range(3):
                                fns.append(lambda e, k=k: e.matmul(ps[6][:, 0:NS], lhsT=dgA[:, k, :], rhs=SaT[:, j, k:48:3],
                                                                   start=False, stop=(k == 2)))
                            P.pe(fns, reads=[('dgA',), ('xasb',), ('SaT',)], writes=[('ps', 6)])
                        sub(40 + ti)
                        P.dve(lambda e: e.tensor_scalar_add(out=xc[:, c0:c0 + n], in0=ps[6][:, 0:n], scalar1=cT[:, j:j + 1]),
                              reads=[('ps', 6), ('cT',)], writes=[('xc', ti)])
                        sub(50 + ti)
                        P.act(lambda e: e.activation(out=xcb[:, c0:c0 + n], in_=ps[6][:, 0:n], func=AF.Identity,
                                                     bias=cT[:, j:j + 1], scale=1.0),
                              reads=[('ps', 6), ('cT',)], writes=[('xcb', ti)])
                        sub(30 + ti)
                        P.pe([lambda e: e.matmul(ps[7][:, 0:n], lhsT=wg[gb_][:, 0, :], rhs=xcb[:, c0:c0 + n], start=True, stop=True)],
                             reads=[('w', 'g', gb_), ('xcb', ti)], writes=[('ps', 7)])
                        P.act(lambda e: e.activation(out=rT[:, c0:c0 + n], in_=ps[7][:, 0:n], func=AF.Sigmoid,
                                                     bias=cT[:, 8 + j:9 + j], scale=1.0),
                              reads=[('ps', 7), ('cT',)], writes=[('rT', ti)])
                        P.pe([lambda e: e.matmul(ps[7][:, 0:n], lhsT=wg[gb_][:, 1, :], rhs=xcb[:, c0:c0 + n], start=True, stop=True)],
                             reads=[('w', 'g', gb_), ('xcb', ti)], writes=[('ps', 7)])
                        P.act(lambda e: e.activation(out=iT[:, c0:c0 + n], in_=ps[7][:, 0:n], func=AF.Sigmoid,
                                                     bias=cT[:, 16 + j:17 + j], scale=1.0),
                              reads=[('ps', 7), ('cT',)], writes=[('iT', ti)])
                    sub(3)
                    nt = len(tl)
                    allr = [('rT', ti) for ti in range(nt)]
                    alli = [('iT', ti) for ti in range(nt)]
                    allx = [('xc', ti) for ti in range(nt)]
                    P.act(lambda e: e.activation(out=aT[:, 0:TS], in_=rT[:, 0:TS], func=AF.Exp, scale=c1[:, j:j + 1]),
                          reads=allr + [('c1',)], writes=[('aT',)])
                    P.act(lambda e: e.activation(out=rT[:, 0:TS], in_=rT[:, 0:TS], func=AF.Exp, scale=c2[:, j:j + 1]),
                          reads=allr + [('c2',)], writes=allr)
                    P.act(lambda e: e.activation(out=rT[:, 0:TS], in_=rT[:, 0:TS], func=AF.Sqrt, bias=oneT[:, 0:1], scale=-1.0),
                          reads=allr + [('oneT',)], writes=allr)
                    P.dve(lambda e: e.tensor_tensor(out=iT[:, 0:TS], in0=iT[:, 0:TS], in1=xc[:, 0:TS], op=ALU.mult),
                          reads=alli + allx, writes=alli)
                    P.dve(lambda e: e.tensor_tensor(out=iT[:, 0:TS], in0=iT[:, 0:TS], in1=rT[:, 0:TS], op=ALU.mult),
                          reads=alli + allr, writes=alli)
                    sub(4)
                    P.dve(lambda e: e.tensor_tensor_scan(out=hh[:, 0:NP], data0=aT[:, 0:NP], data1=iT[:, 0:NP],
                                                         initial=hcT[:, l, j:j + 1], op0=ALU.mult, op1=ALU.add),
                          reads=[('aT',), ('hcT',)] + alli, writes=allx)
                    P.dve(lambda e: e.tensor_copy(out=hcT[:, l, j:j + 1], in_=hh[:, NP - 1:NP]),
                          reads=allx, writes=[('hcT',)])
                    sub(5)
                    if has_s:
                        P.dve(lambda e: e.tensor_tensor(out=hh[:, NP:TT], in0=aT[:, NP:TT], in1=h0T[:, j, :], op=ALU.mult),
                              reads=[('aT',), ('h0T',)], writes=allx)
                        P.dve(lambda e: e.tensor_tensor(out=hh[:, NP:TT], in0=hh[:, NP:TT], in1=iT[:, NP:TT], op=ALU.add),
                              reads=allx + alli, writes=allx)
                        P.dve(lambda e: e.tensor_copy(out=hsF[:, j, :], in_=hh[:, NP:TT]),
                              reads=allx, writes=[('hsF',)])
                    P.dve(lambda e: e.tensor_tensor(out=mixT[:, j, 0:TS], in0=hh[:, 0:TS], in1=gg[:, 0:TS], op=ALU.mult),
                          reads=allx + [('gg',)], writes=[('mixT', j)])
            P.barrier()

        def mixer_b(sb, l, tl, TS, has_s):
            with ExitStack() as ph:
                ub_ext = tmp(ph, "ub_ext", [128, 30 + NP], BF16)
                ubsb = tmp(ph, "ubsb", [128, NS], BF16)
                sg = tmp(ph, "sg", [128, 512], F32)
                yb = tmp(ph, "yb", [128, 4, TT], F32)
                dgB = tmp(ph, "dgB", [128, 31, 128], BF16)
                ysq = [tmp(ph, "ysq", [128, 512], F32) for i in range(2)]
                mean = tmp(ph, "mean", [128, 512], F32)
                rstd = tmp(ph, "rstdb", [128, 512], F32)
                for j in range(4):
                    for k in range(31):
                        P.dve(lambda e: e.tensor_scalar_mul(out=dgB[:, k, :], in0=identb[:],
                                                            scalar1=cbw[:, k * 4 + j:k * 4 + j + 1]),
                              reads=[('identb',), ('cbw',)], writes=[('dgB',)])
                    s_xb = load_w(wview(w_in[l], 0, 2048 + j * 128))
                    s_gb = load_w(wview(w_in[l], 0, 2560 + j * 128))
                    bs1 = next_bset()
                    mm_w(s_xb, hT, hres, bs1, tl)
                    bs2 = next_bset()
                    mm_w(s_gb, hT, hres, bs2, tl)
                    P.dve(lambda e: e.tensor_copy(out=ub_ext[:, 0:30], in_=cbT[:, l, j, :]),
                          reads=[('cbT',)], writes=[('ub_ext',)])
                    for ti, (c0, n) in enumerate(tl):
                        P.act(lambda e: e.activation(out=sg[:, 0:n], in_=ps[bs2[ti]][:, 0:n], func=AF.Sigmoid),
                              reads=[('ps', bs2[ti])], writes=[('sg',)])
                        if ti < 2:
                            P.dve(lambda e: e.tensor_tensor(out=ub_ext[:, 30 + c0:30 + c0 + n], in0=ps[bs1[ti]][:, 0:n],
                                                            in1=sg[:, 0:n], op=ALU.mult),
                                  reads=[('ps', bs1[ti]), ('sg',)], writes=[('ub_ext',)])
                            if ti == 1:
                                P.dve(lambda e: e.tensor_tensor(out=cbT[:, l, j, :], in0=ps[bs1[1]][:, 482:512],
                                                                in1=sg[:, 482:512], op=ALU.mult),
                                      reads=[('ps', bs1[1]), ('sg',)], writes=[('cbT',)])
                        else:
                            P.dve(lambda e: e.tensor_tensor(out=ubsF[:, j, :], in0=ps[bs1[2]][:, 0:NS], in1=sg[:, 0:NS],
                                                            op=ALU.mult),
                                  reads=[('ps', bs1[2]), ('sg',)], writes=[('ubsF',)])
                            P.dve(lambda e: e.tensor_copy(out=ubsb[:], in_=ubsF[:, j, :]),
                                  reads=[('ubsF',)], writes=[('ubsb',)])
                    for ti, (c0, n) in enumerate(tl):
                        bk = 6 + (ti % 2)
                        if ti < 2:
                            fns = [lambda e, k=k: e.matmul(ps[bk][:, 0:n], lhsT=dgB[:, k, :], rhs=ub_ext[:, c0 + k:c0 + k + n],
                                                           start=(k == 0), stop=(k == 30)) for k in range(31)]
                            P.pe(fns, reads=[('dgB',), ('ub_ext',)], writes=[('ps', bk)])
                        else:
                            fns = [lambda e: e.matmul(ps[bk][:, 0:NS], lhsT=dgB[:, 30, :], rhs=ubsb[:], start=True, stop=False)]
                            for k in range(30):
                                fns.append(lambda e, k=k: e.matmul(ps[bk][:, 0:NS], lhsT=dgB[:, k, :], rhs=SbT[:, j, k:480:30],
                                                                   start=False, stop=(k == 29)))
                            P.pe(fns, reads=[('dgB',), ('ubsb',), ('SbT',)], writes=[('ps', bk)])
                        P.act(lambda e: e.activation(out=yb[:, j, c0:c0 + n], in_=ps[bk][:, 0:n], func=AF.Copy),
                              reads=[('ps', bk)], writes=[('yb', j, ti)])
                for ti, (c0, n) in enumerate(tl):
                    for j in range(4):
                        yq = ysq[j % 2]
                        P.act(lambda e: e.activation(out=yq[:, 0:n], in_=yb[:, j, c0:c0 + n], func=AF.Square),
                              reads=[('yb', j, ti)], writes=[('ysq', j % 2)])
                        P.pe([lambda e: e.matmul(ps[6][:, 0:n], lhsT=onesf[:], rhs=yb[:, j, c0:c0 + n],
                                                 start=(j == 0), stop=(j == 3)),
                              lambda e: e.matmul(ps[7][:, 0:n], lhsT=onesf[:], rhs=yq[:, 0:n],
                                                 start=(j == 0), stop=(j == 3))],
                             reads=[('yb', j, ti), ('ysq', j % 2), ('onesf',)], writes=[('ps', 6), ('ps', 7)])
                    P.act(lambda e: e.activation(out=mean[:, 0:n], in_=ps[6][:, 0:n], func=AF.Copy, scale=1.0 / 512),
                          reads=[('ps', 6)], writes=[('mean',)])
                    P.dve(lambda e: e.tensor_tensor(out=rstd[:, 0:n], in0=mean[:, 0:n], in1=mean[:, 0:n], op=ALU.mult),
                          reads=[('mean',)], writes=[('rstdb',)])
                    P.dve(lambda e: e.scalar_tensor_tensor(out=rstd[:, 0:n], in0=ps[7][:, 0:n], scalar=1.0 / 512,
                                                           in1=rstd[:, 0:n], op0=ALU.mult, op1=ALU.subtract),
                          reads=[('ps', 7), ('rstdb',)], writes=[('rstdb',)])
                    P.act(lambda e: e.activation(out=rstd[:, 0:n], in_=rstd[:, 0:n], func=AF.Sqrt, bias=epsT[:, 0:1], scale=1.0),
                          reads=[('rstdb',), ('epsT',)], writes=[('rstdb',)])
                    P.dve(lambda e: e.reciprocal(out=rstd[:, 0:n], in_=rstd[:, 0:n]),
                          reads=[('rstdb',)], writes=[('rstdb',)])
                    for j in range(4):
                        P.dve(lambda e: e.tensor_tensor(out=yb[:, j, c0:c0 + n], in0=yb[:, j, c0:c0 + n], in1=mean[:, 0:n],
                                                        op=ALU.subtract),
                              reads=[('yb', j, ti), ('mean',)], writes=[('yb', j, ti)])
                        P.dve(lambda e: e.tensor_tensor(out=yb[:, j, c0:c0 + n], in0=yb[:, j, c0:c0 + n], in1=rstd[:, 0:n],
                                                        op=ALU.mult),
                              reads=[('yb', j, ti), ('rstdb',)], writes=[('yb', j, ti)])
                        P.act(lambda e: e.activation(out=mixT[:, 8 + j, c0:c0 + n], in_=yb[:, j, c0:c0 + n], func=AF.Silu,
                                                     bias=cT[:, 36 + j:37 + j], scale=cT[:, 32 + j:33 + j]),
                              reads=[('yb', j, ti), ('cT',)], writes=[('mixT', 8 + j)])
            P.barrier()

        def mixer_c(sb, l, tl, TS, has_s):
            with ExitStack() as ph:
                ucg = tmp(ph, "ucg", [128, 4, TT], BF16)
                vg = [tmp(ph, "vg", [128, 512], F32) for i in range(2)]
                vnb = [tmp(ph, "vnb", [128, 512], BF16) for i in range(2)]
                st6 = tmp(ph, "st6", [128, 6], F32)
                mv = tmp(ph, "mv", [128, 2], F32)
                tmpc = tmp(ph, "tmpc", [128, 128], F32)
                sgs = tmp(ph, "sgs", [128, 4, 128], F32)
                Wt = tmp(ph, "Wt", [128, 4, 128], BF16)
                bsb = tmp(ph, "bsb", [128, 4, 128], F32)
                gbc = tmp(ph, "gbc", [128, 512], F32)
                bbc = tmp(ph, "bbc", [128, 512], F32)
                w00T = tmp(ph, "w00T", [128, 4], F32)
                b0T = tmp(ph, "b0T", [128, 4], F32)
                rhs_s = tmp(ph, "rhs_s", [16, 4, 16], BF16)
                fns = [lambda e: e.dma_start(out=sgs[:], in_=sgu_w[l].rearrange("h t s -> t h s")),
                       lambda e: e.dma_start(out=bsb[:], in_=sgu_b[l].partition_broadcast(128)),
                       lambda e: e.dma_start(out=gbc[:], in_=sgu_ln_g[l].partition_broadcast(128)),
                       lambda e: e.dma_start(out=bbc[:], in_=sgu_ln_b[l].partition_broadcast(128))]
                for h in range(4):
                    fns.append(lambda e, h=h: e.dma_start(out=w00T[:, h:h + 1], in_=sgu_w[l, h, 0, 0:1].partition_broadcast(128)))
                    fns.append(lambda e, h=h: e.dma_start(out=b0T[:, h:h + 1], in_=sgu_b[l, h, 0:1].partition_broadcast(128)))
                P.dma('sp', fns, writes=[('sgs',), ('bsb',), ('gbc',), ('bbc',), ('w00T',), ('b0T',)])
                for h in range(4):
                    bk = 6 + (h % 2)
                    P.pe([tr(sgs[:, h, :], 128, ps[bk][:, 0:128])], reads=[('sgs',), ('ident',)], writes=[('ps', bk)])
                    P.dve(lambda e: e.tensor_tensor(out=Wt[:, h, :], in0=ps[bk][:, 0:128], in1=tri[:], op=ALU.mult),
                          reads=[('ps', bk), ('tri',)], writes=[('Wt',)])
                    P.dve(lambda e: e.tensor_scalar_mul(out=rhs_s[:, h, :], in0=ident[0:16, 0:16], scalar1=w00T[0:16, h:h + 1]),
                          reads=[('w00T',), ('ident',)], writes=[('rhs_s',)])
                for j in range(4):
                    s_uc = load_w(wview(w_in[l], 0, 3072 + j * 128))
                    bs1 = next_bset()
                    mm_w(s_uc, hT, hres, bs1, tl)
                    for ti, (c0, n) in enumerate(tl):
                        P.act(lambda e: e.activation(out=ucg[:, j, c0:c0 + n], in_=ps[bs1[ti]][:, 0:n], func=GELU_FUNC),
                              reads=[('ps', bs1[ti])], writes=[('ucg', j)])
                s_vc = [load_w(wview(w_in[l], 0, 3584 + j * 128)) for j in range(4)]
                ttl = [(t * 128, 128) for t in range(8)] + ([(NP, NS)] if has_s else [])
                def stage1(tix, c0, n):
                    b = tix % 2
                    bk = 6 + b
                    fns = []
                    for j in range(4):
                        for k in range(KD):
                            fns.append(lambda e, j=j, k=k: e.matmul(
                                ps[bk][0:n, j * 128:(j + 1) * 128], lhsT=hT[:, k, c0:c0 + n], rhs=wring[s_vc[j]][:, k, :],
                                start=(k == 0), stop=(k == KD - 1)))
                    P.pe(fns, reads=hres + [('w', s) for s in s_vc], writes=[('ps', bk)])
                    P.act(lambda e: e.activation(out=vg[b][0:n, :], in_=ps[bk][0:n, :], func=GELU_FUNC),
                          reads=[('ps', bk)], writes=[('vg', b)])
                    P.dve(lambda e: e.bn_stats(out=st6[0:n, :], in_=vg[b][0:n, :]), reads=[('vg', b)], writes=[('st6',)])
                    P.dve(lambda e: e.bn_aggr(out=mv[0:n, :], in_=st6[0:n, :]), reads=[('st6',)], writes=[('mv',)])
                    P.act(lambda e: e.activation(out=mv[0:n, 1:2], in_=mv[0:n, 1:2], func=AF.Sqrt, bias=epsT[0:n, 0:1], scale=1.0),
                          reads=[('mv',), ('epsT',)], writes=[('mv',)])
                    P.dve(lambda e: e.reciprocal(out=mv[0:n, 1:2], in_=mv[0:n, 1:2]),
                          reads=[('mv',)], writes=[('mv',)])
                    P.dve(lambda e: e.tensor_scalar(out=vg[b][0:n, :], in0=vg[b][0:n, :], scalar1=mv[0:n, 0:1],
                                                    scalar2=mv[0:n, 1:2], op0=ALU.subtract, op1=ALU.mult),
                          reads=[('vg', b), ('mv',)], writes=[('vg', b)])
                    P.dve(lambda e: e.tensor_tensor(out=vg[b][0:n, :], in0=vg[b][0:n, :], in1=gbc[0:n, :], op=ALU.mult),
                          reads=[('vg', b), ('gbc',)], writes=[('vg', b)])
                    P.dve(lambda e: e.tensor_tensor(out=vg[b][0:n, :], in0=vg[b][0:n, :], in1=bbc[0:n, :], op=ALU.add),
                          reads=[('vg', b), ('bbc',)], writes=[('vg', b)])
                    P.act(lambda e: e.activation(out=vnb[b][0:n, :], in_=vg[b][0:n, :], func=AF.Copy),
                          reads=[('vg', b)], writes=[('vnb', b)])
                    if n == NS:
                        P.dma('sp', [lambda e: e.dma_start(out=vs[l], in_=vg[b][0:NS, :])],
                              reads=[('vg', b)], writes=[('o_vs', l)])

                def stage2(tix, c0, n):
                    b = tix % 2
                    for h in range(4):
                        bk2 = SA[h % 3] if (h + tix) % 2 == 0 else SB[h % 3]
                        if n == 128:
                            P.pe([lambda e: e.matmul(ps[bk2][:, 0:128], lhsT=vnb[b][:, h * 128:(h + 1) * 128],
                                                     rhs=Wt[:, h, :], start=True, stop=True)],
                                 reads=[('vnb', b), ('Wt',)], writes=[('ps', bk2)])
                            P.dve(lambda e: e.tensor_tensor(out=tmpc[:], in0=ps[bk2][:, 0:128], in1=bsb[:, h, :], op=ALU.add),
                                  reads=[('ps', bk2), ('bsb',)], writes=[('tmpc',)])
                            P.dve(lambda e: e.tensor_tensor(out=mixT[:, 12 + h, c0:c0 + 128], in0=tmpc[:],
                                                            in1=ucg[:, h, c0:c0 + 128], op=ALU.mult),
                                  reads=[('tmpc',), ('ucg', h)], writes=[('mixT', 12 + h)])
                        else:
                            P.pe([lambda e: e.matmul(ps[bk2][:, 0:NS], lhsT=vnb[b][0:NS, h * 128:(h + 1) * 128],
                                                     rhs=rhs_s[:, h, :], start=True, stop=True)],
                                 reads=[('vnb', b), ('rhs_s',)], writes=[('ps', bk2)])
                            P.dve(lambda e: e.scalar_tensor_tensor(
                                out=mixT[:, 12 + h, NP:TT], in0=ps[bk2][:, 0:NS], scalar=b0T[:, h:h + 1],
                                in1=ucg[:, h, NP:TT], op0=ALU.add, op1=ALU.mult),
                                reads=[('ps', bk2), ('b0T',), ('ucg', h)], writes=[('mixT', 12 + h)])

                stage1(0, *ttl[0])
                for tix, (c0, n) in enumerate(ttl):
                    if tix + 1 < len(ttl):
                        stage1(tix + 1, *ttl[tix + 1])
                    stage2(tix, c0, n)
            P.barrier()

        def out_proj(sb, l, tl, TS, has_s):
            for dblk in range(KD):
                s = load_w(wview(w_out[l], 0, dblk * 128))
                bs1 = next_bset()
                mm_w(s, mixT, mres, bs1, tl)
                for ti, (c0, n) in enumerate(tl):
                    P.dve(lambda e: e.tensor_tensor(out=xT[:, dblk, c0:c0 + n], in0=ps[bs1[ti]][:, 0:n],
                                                    in1=xT[:, dblk, c0:c0 + n], op=ALU.add),
                          reads=[('ps', bs1[ti]), ('xT', dblk)], writes=[('xT', dblk)])
            P.barrier()
            if has_s:
                with ExitStack() as ph:
                    ost = tmp(ph, "ost", [128, 3, 128], F32)
                    P.pe([tr(xasF[:].rearrange("p j b -> p (j b)"), 128, ps[6][:, 0:128])],
                         reads=[('xasF',), ('ident',)], writes=[('ps', 6)])
                    P.act(lambda e: e.activation(out=ost[:, 0, :], in_=ps[6][:, 0:128], func=AF.Copy),
                          reads=[('ps', 6)], writes=[('ost', 0)])
                    P.dma('sp', [lambda e, j=j: e.dma_start(out=cas[l][:, 2, j * 128:(j + 1) * 128],
                                                            in_=ost[j * NS:(j + 1) * NS, 0, :]) for j in range(8)],
                          reads=[('ost', 0)], writes=[('o_cas', l)])
                    P.pe([tr(hsF[:].rearrange("p j b -> p (j b)"), 128, ps[7][:, 0:128])],
                         reads=[('hsF',), ('ident',)], writes=[('ps', 7)])
                    P.act(lambda e: e.activation(out=ost[:, 1, :], in_=ps[7][:, 0:128], func=AF.Copy),
                          reads=[('ps', 7)], writes=[('ost', 1)])
                    P.dma('sp', [lambda e, j=j: e.dma_start(out=hs[l][:, j * 128:(j + 1) * 128],
                                                            in_=ost[j * NS:(j + 1) * NS, 1, :]) for j in range(8)],
                          reads=[('ost', 1)], writes=[('o_hs', l)])
                    P.pe([tr(ubsF[:].rearrange("p j b -> p (j b)"), 128, ps[6][0:64, 0:128])],
                         reads=[('ubsF',), ('ident',)], writes=[('ps', 6)])
                    P.act(lambda e: e.activation(out=ost[0:64, 2, :], in_=ps[6][0:64, 0:128], func=AF.Copy),
                          reads=[('ps', 6)], writes=[('ost', 2)])
                    P.dma('sp', [lambda e, j=j: e.dma_start(out=cbs[l][:, 29, j * 128:(j + 1) * 128],
                                                            in_=ost[j * NS:(j + 1) * NS, 2, :]) for j in range(4)],
                          reads=[('ost', 2)], writes=[('o_cbs', l)])
                P.barrier()

        def ffn(sb, l, tl, TS, has_s):
            with ExitStack() as ph:
                rl = [tmp(ph, "rl", [128, TT], F32) for i in range(2)]
                ffT = mixT
                for g in range(4):
                    for fb in range(16):
                        s = load_w(wview(w_ff1[l], 0, (g * 16 + fb) * 128))
                        bs1 = next_bset()
                        mm_w(s, hT, hres, bs1, tl)
                        b = fb % 2
                        for ti, (c0, n) in enumerate(tl):
                            P.act(lambda e: e.activation(out=rl[b][:, c0:c0 + n], in_=ps[bs1[ti]][:, 0:n], func=AF.Relu),
                                  reads=[('ps', bs1[ti])], writes=[('rl', b)])
                        P.dve(lambda e: e.tensor_tensor(out=ffT[:, fb, 0:TS], in0=rl[b][:, 0:TS], in1=rl[b][:, 0:TS],
                                                        op=ALU.mult),
                              reads=[('rl', b)], writes=[('mixT', fb)])
                    for dblk in range(KD):
                        s = load_w(wview(w_ff2[l], g * 2048, dblk * 128))
                        bs1 = next_bset()
                        mm_w(s, ffT, mres, bs1, tl)
                        for ti, (c0, n) in enumerate(tl):
                            P.dve(lambda e: e.tensor_tensor(out=xT[:, dblk, c0:c0 + n], in0=ps[bs1[ti]][:, 0:n],
                                                            in1=xT[:, dblk, c0:c0 + n], op=ALU.add),
                                  reads=[('ps', bs1[ti]), ('xT', dblk)], writes=[('xT', dblk)])
            P.barrier()

        def load_tokens(sb, has_s):
            with ExitStack() as ph:
                stg = [tmp(ph, "stg", [128, D], F32) for i in range(2)]

                def load_T(rows_ap, n, col0, bi):
                    P.dma('sp', [lambda e: e.dma_start(out=stg[bi][0:n, :], in_=rows_ap)], writes=[('stg', bi)])
                    for q in range(4):
                        bk = 6 + (q % 2)
                        fns = [tr(stg[bi][0:n, (4 * q + i) * 128:(4 * q + i + 1) * 128], n,
                                  ps[bk][:, i * n:(i + 1) * n]) for i in range(4)]
                        P.pe(fns, reads=[('stg', bi), ('ident',)], writes=[('ps', bk)])
                        P.act(lambda e: e.activation(out=xT[:, 4 * q:4 * q + 4, col0:col0 + n],
                                                     in_=ps[bk][:, 0:4 * n].rearrange("p (i n) -> p i n", n=n), func=AF.Copy),
                              reads=[('ps', bk)], writes=xres(range(4 * q, 4 * q + 4)))
                for tt in range(8):
                    r0 = sb * NP + tt * 128
                    load_T(xp[r0:r0 + 128, :], 128, tt * 128, tt % 2)
                if has_s:
                    load_T(xs[:, :], NS, NP, 0)
            P.barrier()

        def store_tokens(sb, has_s):
            with ExitStack() as ph:
                ost = [tmp(ph, "yst", [128, D], F32) for i in range(2)]

                def store_T(rows_ap, n, col0, bi):
                    for q in range(4):
                        bk = 6 + (q % 2)
                        fns = [tr(xT[:, 4 * q + i, col0:col0 + n], 128, ps[bk][0:n, i * 128:(i + 1) * 128]) for i in range(4)]
                        P.pe(fns, reads=xres(range(4 * q, 4 * q + 4)) + [('ident',)], writes=[('ps', bk)])
                        P.act(lambda e: e.activation(out=ost[bi][0:n, q * 512:(q + 1) * 512], in_=ps[bk][0:n, :], func=AF.Copy),
                              reads=[('ps', bk)], writes=[('yst', bi)])
                    P.dma('sp', [lambda e: e.dma_start(out=rows_ap, in_=ost[bi][0:n, :])], reads=[('yst', bi)],
                          writes=[('o_y', col0, sb)])
                for tt in range(8):
                    r0 = sb * NP + tt * 128
                    store_T(yp[r0:r0 + 128, :], 128, tt * 128, tt % 2)
                if has_s:
                    store_T(ys[:, :], NS, NP, 0)
            P.barrier()

        kstop = 0
        pc = [0]

        def chk():
            pc[0] += 1
            if kstop and pc[0] >= kstop:
                _DEAD[0] = True
        try:
            for sb in range(2):
                has_s = (sb == 0)
                tl = [(0, 512), (512, 512)] + ([(NP, NS)] if has_s else [])
                TS = TT if has_s else NP
                load_tokens(sb, has_s)
                chk()
                for l in range(2):
                    do_consts(sb, l, has_s)
                    chk()
                    rmsnorm(lambda k: cT[:, 40 + k:41 + k], True, tl, TS)
                    chk()
                    mixer_a(sb, l, tl, TS, has_s)
                    chk()
                    mixer_b(sb, l, tl, TS, has_s)
                    chk()
                    mixer_c(sb, l, tl, TS, has_s)
                    chk()
                    out_proj(sb, l, tl, TS, has_s)
                    chk()
                    rmsnorm(lambda k: cT[:, 56 + k:57 + k], True, tl, TS)
                    ffn(sb, l, tl, TS, has_s)
                    chk()
                rmsnorm(lambda k: gfT[:, k:k + 1], False, tl, TS)
                store_tokens(sb, has_s)
                chk()
        except _Stop:
            pass

        with ExitStack() as ph:
            ost = tmp(ph, "tst", [128, 3, 128], F32)
            hpad = tmp(ph, "hpad", [128, 32], F32)
            for l in range(2):
                P.pe([tr(caT[:, l, :, :].rearrange("p j k -> p (j k)"), 128, ps[6][0:24, 0:128])],
                     reads=[('caT',), ('ident',)], writes=[('ps', 6)])
                P.act(lambda e: e.activation(out=ost[0:24, 0, :], in_=ps[6][0:24, 0:128], func=AF.Copy),
                      reads=[('ps', 6)], writes=[('tst', 0)])
                P.dma('sp', [lambda e, j=j: e.dma_start(out=cap[l][:, j * 128:(j + 1) * 128], in_=ost[j * 3:(j + 1) * 3, 0, :])
                             for j in range(8)],
                      reads=[('tst', 0)], writes=[('o_cap', l)])
                sub(60 + 3 * l)
                P.pe([tr(cbT[:, l, :, :].rearrange("p j k -> p (j k)"), 128, ps[7][0:120, 0:128])],
                     reads=[('cbT',), ('ident',)], writes=[('ps', 7)])
                P.act(lambda e: e.activation(out=ost[0:120, 1, :], in_=ps[7][0:120, 0:128], func=AF.Copy),
                      reads=[('ps', 7)], writes=[('tst', 1)])
                P.dma('sp', [lambda e, j=j: e.dma_start(out=cbp[l][:, j * 128:(j + 1) * 128], in_=ost[j * 30:(j + 1) * 30, 1, :])
                             for j in range(4)],
                      reads=[('tst', 1)], writes=[('o_cbp', l)])
                sub(61 + 3 * l)
                P.dve(lambda e: e.memset(hpad[:], 0.0), writes=[('hpad',)])
                P.dve(lambda e: e.tensor_copy(out=hpad[:, 0:8], in_=hcT[:, l, :]), reads=[('hcT',)], writes=[('hpad',)])
                P.pe([tr(hpad[:], 128, ps[6][0:32, 0:128])], reads=[('hpad',), ('ident',)], writes=[('ps', 6)])
                P.act(lambda e: e.activation(out=ost[0:8, 2, :], in_=ps[6][0:8, 0:128], func=AF.Copy),
                      reads=[('ps', 6)], writes=[('tst', 2)])
                P.dma('sp', [lambda e: e.dma_start(out=hp[l].rearrange("(j c) -> j c", c=128), in_=ost[0:8, 2, :])],
                      reads=[('tst', 2)], writes=[('o_hp', l)])
            P.final_wait('sp')

            with nc.Block() as block:
                @block.tensor
                def _(e):
                    _replay(e, P.streams['pe'])

                @block.scalar
                def _(e):
                    _replay(e, P.streams['act'])

                @block.vector
                def _(e):
                    _replay(e, P.streams['dve'])

                @block.gpsimd
                def _(e):
                    _replay(e, P.streams['pool'])

                @block.sync
                def _(e):
                    _replay(e, P.streams['sp'])
    return nc


_NC_CACHE = {}

WEIGHT_KEYS = ['norm_mix', 'w_in', 'conv_a_w', 'conv_a_b', 'gate_r_w', 'gate_r_b', 'gate_i_w', 'gate_i_b',
               'lru_lambda', 'conv_b_w', 'ln_b_g', 'ln_b_b', 'sgu_ln_g', 'sgu_ln_b', 'sgu_w', 'sgu_b',
               'w_out', 'norm_ffn', 'w_ff1', 'w_ff2', 'norm_final']


def kernel(**inputs):
    f = lambda a: np.ascontiguousarray(np.asarray(a, dtype=np.float32))
    x_prompt = f(inputs['x_prompt'])
    x_sample = f(inputs['x_sample'])
    sca = f(inputs['state_conv_a'])
    slh = f(inputs['state_lru_h'])
    scb = f(inputs['state_conv_b'])
    W = {k: f(inputs[k]) for k in WEIGHT_KEYS}
    if 'nc' not in _NC_CACHE:
        _NC_CACHE['nc'] = build_nc()
    nc = _NC_CACHE['nc']
    in_maps = []
    for c in range(8):
        b0 = c * NS
        m = {'xp': x_prompt[c % 4], 'xs': f(x_sample[b0:b0 + NS, 0, :]),
             'sca': f(sca[:, b0:b0 + NS]), 'slh': f(slh[:, b0:b0 + NS]), 'scb': f(scb[:, b0:b0 + NS])}
        m.update(W)
        in_maps.append(m)
    res = run_bass_kernel_spmd(nc, in_maps, core_ids=list(range(8)))
    R = res.results
    y_prompt = np.stack([np.asarray(R[c]['yp']) for c in range(4)], axis=0)
    y_sample = np.concatenate([np.asarray(R[c]['ys']) for c in range(8)], axis=0)[:, None, :]
    ca_p = np.stack([np.asarray(R[c]['cap']) for c in range(4)], axis=1)
    h_p = np.stack([np.asarray(R[c]['hp']) for c in range(4)], axis=1)
    cb_p = np.stack([np.asarray(R[c]['cbp']) for c in range(4)], axis=1)
    ca_s = np.concatenate([np.asarray(R[c]['cas']) for c in range(8)], axis=1)
    h_s = np.concatenate([np.asarray(R[c]['hs']) for c in range(8)], axis=1)
    cb_s = np.concatenate([np.asarray(R[c]['cbs']) for c in range(8)], axis=1)
    v_s = np.concatenate([np.asarray(R[c]['vs']) for c in range(8)], axis=1)[:, :, None, :]
    outs = (y_prompt, y_sample, ca_p, h_p, cb_p, ca_s, h_s, cb_s, v_s)
    return tuple(np.ascontiguousarray(o.astype(np.float32)) for o in outs)
```
